# Optimizing a Trainium2 kernel written in Bass

```python
import jax
import jax.numpy as jnp
from jax import lax
import numpy as np

D_MODEL = 2048
BATCH = 4
SEQ = 4096
DEPTH = 1

CHUNK = 64
Q_BLOCK = 128
NORM_EPS = 1e-6
NEG_INF = -1e30

FOX_HEADS = 8
FOX_HEAD_DIM = D_MODEL // (2 * FOX_HEADS)
FOX_WIDTH = FOX_HEADS * FOX_HEAD_DIM

MLSTM_HEADS = 8
MLSTM_V_DIM = D_MODEL // (2 * MLSTM_HEADS)
MLSTM_QK_DIM = MLSTM_V_DIM // 2
MLSTM_V_WIDTH = MLSTM_HEADS * MLSTM_V_DIM
MLSTM_QK_WIDTH = MLSTM_HEADS * MLSTM_QK_DIM
CONV_WIDTH = 4

MIX_WIDTH = FOX_WIDTH + MLSTM_V_WIDTH

IN_SIZES = (FOX_WIDTH, FOX_WIDTH, FOX_WIDTH, FOX_WIDTH, FOX_HEADS,
            MLSTM_QK_WIDTH, MLSTM_QK_WIDTH, MLSTM_V_WIDTH, MLSTM_V_WIDTH, MLSTM_V_WIDTH,
            MLSTM_HEADS, MLSTM_HEADS)
IN_COLS = sum(IN_SIZES)

kernel_name = 'fox_mlstm_parallel_heads_block'


def rms_norm(x, g):
    xf = x.astype(jnp.float32)
    y = xf * lax.rsqrt(jnp.mean(xf * xf, axis=-1, keepdims=True) + NORM_EPS)
    return (y * g.astype(jnp.float32)).astype(x.dtype)


def head_rms_norm(y, g):
    h, d = y.shape[1], y.shape[3]
    yf = y.astype(jnp.float32)
    yf = yf * lax.rsqrt(jnp.mean(yf * yf, axis=-1, keepdims=True) + NORM_EPS)
    return (yf * g.reshape(h, d)[None, :, None, :].astype(jnp.float32)).astype(y.dtype)


def to_heads(t, n_heads):
    b, s, w = t.shape
    return t.reshape(b, s, n_heads, w // n_heads).transpose(0, 2, 1, 3)


def from_heads(t):
    b, h, s, d = t.shape
    return t.transpose(0, 2, 1, 3).reshape(b, s, h * d)


def causal_dwconv(u, w, bias):
    s = u.shape[1]
    up = jnp.pad(u, ((0, 0), (CONV_WIDTH - 1, 0), (0, 0)))
    y = up[:, 0:s] * w[0]
    for j in range(1, CONV_WIDTH):
        y = y + up[:, j:j + s] * w[j]
    return y + bias


def fox_attention(q, k, v, f_pre):
    b, h, s, d = q.shape
    cum_logf = jnp.cumsum(jax.nn.log_sigmoid(f_pre), axis=-1)
    scale = d ** -0.5
    kpos = jnp.arange(s)

    def block(i):
        start = i * Q_BLOCK
        qb = lax.dynamic_slice_in_dim(q, start, Q_BLOCK, axis=2)
        fq = lax.dynamic_slice_in_dim(cum_logf, start, Q_BLOCK, axis=2)
        logits = jnp.einsum('bhqd,bhkd->bhqk', qb, k).astype(jnp.float32) * scale
        logits = logits + fq[..., :, None] - cum_logf[..., None, :]
        qpos = start + jnp.arange(Q_BLOCK)
        mask = kpos[None, :] <= qpos[:, None]
        p = jax.nn.softmax(jnp.where(mask, logits, NEG_INF), axis=-1)
        return jnp.einsum('bhqk,bhkd->bhqd', p.astype(v.dtype), v)

    out = lax.map(block, jnp.arange(s // Q_BLOCK))
    return out.transpose(1, 2, 0, 3, 4).reshape(b, h, s, d)


def mlstm_chunkwise(q, k, v, i_pre, log_f):
    b, h, s, dk = q.shape
    dv = v.shape[-1]
    nc = s // CHUNK

    def chunks(t):
        return jnp.moveaxis(t.reshape(b, h, nc, CHUNK, *t.shape[3:]), 2, 0)

    causal = jnp.tril(jnp.ones((CHUNK, CHUNK), dtype=bool))

    def body(carry, inp):
        c_prev, n_prev, m_prev = carry
        qc, kc, vc, ic, fc = inp
        bcum = jnp.cumsum(fc, axis=-1)
        dmat = bcum[..., :, None] - bcum[..., None, :] + ic[..., None, :]
        dmat = jnp.where(causal, dmat, NEG_INF)
        inter = bcum + m_prev[..., None]
        m_t = jnp.maximum(inter, jnp.max(dmat, axis=-1))
        w_intra = jnp.exp(dmat - m_t[..., None])
        w_inter = jnp.exp(inter - m_t)
        sc = jnp.einsum('bhtd,bhsd->bhts', qc, kc) * w_intra
        num = (jnp.einsum('bhts,bhsv->bhtv', sc, vc)
               + w_inter[..., None] * jnp.einsum('bhtd,bhvd->bhtv', qc, c_prev))
        den = jnp.sum(sc, axis=-1) + w_inter * jnp.einsum('bhtd,bhd->bht', qc, n_prev)
        h_t = num / jnp.maximum(jnp.abs(den), jnp.exp(-m_t))[..., None]
        b_last = bcum[..., -1]
        w_s = b_last[..., None] - bcum + ic
        m_new = jnp.maximum(b_last + m_prev, jnp.max(w_s, axis=-1))
        decay = jnp.exp(b_last + m_prev - m_new)
        w_s = jnp.exp(w_s - m_new[..., None])
        c_new = decay[..., None, None] * c_prev + jnp.einsum('bhs,bhsv,bhsd->bhvd', w_s, vc, kc)
        n_new = decay[..., None] * n_prev + jnp.einsum('bhs,bhsd->bhd', w_s, kc)
        return (c_new, n_new, m_new), h_t

    init = (jnp.zeros((b, h, dv, dk), jnp.float32),
            jnp.zeros((b, h, dk), jnp.float32),
            jnp.zeros((b, h), jnp.float32))
    _, hs = lax.scan(body, init, (chunks(q), chunks(k), chunks(v), chunks(i_pre), chunks(log_f)))
    return jnp.moveaxis(hs, 0, 2).reshape(b, h, s, dv)


def setup_inputs(seed: int = 0) -> dict:
    key = jax.random.key(seed)
    ks = jax.random.split(key, 13)
    f32 = jnp.float32
    nrm = jax.random.normal
    x = nrm(ks[0], (BATCH, SEQ, D_MODEL), f32)
    norm_w = 1.0 + 0.02 * nrm(ks[1], (DEPTH, D_MODEL), f32)
    w_in = nrm(ks[2], (DEPTH, D_MODEL, IN_COLS), f32) * D_MODEL ** -0.5
    fox_f_bias = jnp.linspace(2.0, 5.0, FOX_HEADS, dtype=f32)[None, :] + 0.1 * nrm(ks[3], (DEPTH, FOX_HEADS), f32)
    conv_w = nrm(ks[4], (DEPTH, CONV_WIDTH, 2 * MLSTM_QK_WIDTH), f32) * CONV_WIDTH ** -0.5
    conv_b = 0.01 * nrm(ks[5], (DEPTH, 2 * MLSTM_QK_WIDTH), f32)
    mlstm_i_bias = -3.0 + 0.1 * nrm(ks[6], (DEPTH, MLSTM_HEADS), f32)
    mlstm_f_bias = jnp.linspace(3.0, 6.0, MLSTM_HEADS, dtype=f32)[None, :] + 0.1 * nrm(ks[7], (DEPTH, MLSTM_HEADS), f32)
    fox_out_norm_w = 1.0 + 0.02 * nrm(ks[8], (DEPTH, FOX_WIDTH), f32)
    mlstm_out_norm_w = 1.0 + 0.02 * nrm(ks[9], (DEPTH, MLSTM_V_WIDTH), f32)
    w_out = nrm(ks[10], (DEPTH, MIX_WIDTH, D_MODEL), f32) * MIX_WIDTH ** -0.5
    final_norm_w = 1.0 + 0.02 * nrm(ks[11], (D_MODEL,), f32)
    return {'x': x, 'norm_w': norm_w, 'w_in': w_in, 'fox_f_bias': fox_f_bias,
            'conv_w': conv_w, 'conv_b': conv_b, 'mlstm_i_bias': mlstm_i_bias,
            'mlstm_f_bias': mlstm_f_bias, 'fox_out_norm_w': fox_out_norm_w,
            'mlstm_out_norm_w': mlstm_out_norm_w, 'w_out': w_out, 'final_norm_w': final_norm_w}


def reference(x, norm_w, w_in, fox_f_bias, conv_w, conv_b, mlstm_i_bias, mlstm_f_bias,
              fox_out_norm_w, mlstm_out_norm_w, w_out, final_norm_w):
    f32 = jnp.float32
    split_idx = [int(v) for v in np.cumsum(IN_SIZES)[:-1]]
    for l in range(DEPTH):
        h = rms_norm(x, norm_w[l])
        proj = jnp.einsum('bsd,de->bse', h, w_in[l])
        (fq, fk, fv, fz, ff, mq, mk, mv, mo, mz, mi, mf) = jnp.split(proj, split_idx, axis=-1)

        f_pre = (ff + fox_f_bias[l]).astype(f32).transpose(0, 2, 1)
        ya = fox_attention(to_heads(fq, FOX_HEADS), to_heads(fk, FOX_HEADS),
                           to_heads(fv, FOX_HEADS), f_pre)
        ya = from_heads(head_rms_norm(ya, fox_out_norm_w[l])) * jax.nn.silu(fz)

        qk = jax.nn.silu(causal_dwconv(jnp.concatenate([mq, mk], axis=-1), conv_w[l], conv_b[l]))
        q_m = to_heads(qk[..., :MLSTM_QK_WIDTH], MLSTM_HEADS).astype(f32)
        k_m = to_heads(qk[..., MLSTM_QK_WIDTH:], MLSTM_HEADS).astype(f32) * MLSTM_QK_DIM ** -0.5
        v_m = to_heads(mv, MLSTM_HEADS).astype(f32)
        i_pre = (mi + mlstm_i_bias[l]).astype(f32).transpose(0, 2, 1)
        log_f = jax.nn.log_sigmoid((mf + mlstm_f_bias[l]).astype(f32)).transpose(0, 2, 1)
        hb = mlstm_chunkwise(q_m, k_m, v_m, i_pre, log_f)
        hb = (hb * jax.nn.sigmoid(to_heads(mo, MLSTM_HEADS).astype(f32))).astype(x.dtype)
        yb = from_heads(head_rms_norm(hb, mlstm_out_norm_w[l])) * jax.nn.silu(mz)

        y = jnp.concatenate([ya, yb], axis=-1)
        x = x + jnp.einsum('bse,ed->bsd', y, w_out[l])
    return rms_norm(x, final_norm_w)
```

```python
import os
import numpy as np
import ml_dtypes
import concourse.bass as bass
import concourse.mybir as mybir
from concourse.bass_utils import run_bass_kernel_spmd

F32 = mybir.dt.float32
BF16 = mybir.dt.bfloat16
AF = mybir.ActivationFunctionType
ALU = mybir.AluOpType

T = 4096
D = 2048
KC = 16
TO = 2048
EPS = 1e-6
C_FQ, C_FK, C_FV, C_FZ, C_FF = 0, 1024, 2048, 3072, 4096
C_MQ, C_MK, C_MV, C_MO, C_MZ, C_MI, C_MF = 4104, 4616, 5128, 6152, 7176, 8200, 8208
NCOL = 8216
LN8 = float(np.log(8.0))

V_GN = 0
V_GF = V_GN + 2048
V_GFOX = V_GF + 2048
V_GML = V_GFOX + 1024
V_GB = V_GML + 1024
V_CW = V_GB + 768
V_CB = V_CW + 32
V_SEL = V_CB + 8
V_KILL = V_SEL + 2
V_TRI = V_KILL + 8
V_ONE = V_TRI + 128
CV = V_ONE + 128
M_ID = 0
M_MASK = 128
CM = M_MASK + 8 * 512

DEBUG = bool(int(os.environ.get("KDEBUG", "0")))
STOP_AFTER = os.environ.get("KSTOP", "")


class Ins:
    __slots__ = ("stream", "fn", "deps", "signal", "is_dma", "dkey", "sem", "thresh")


class Sched:
    def __init__(self, nc):
        self.nc = nc
        self.E = dict(pe=nc.tensor, act=nc.scalar, dve=nc.vector, pool=nc.gpsimd, sp=nc.sync)
        self.csem = {s: nc.alloc_semaphore("c_" + s) for s in ("pe", "act", "dve", "pool")}
        self.ccnt = dict.fromkeys(self.csem, 0)
        self.dsem = {}
        self.pending = []
        self.lastw = {}
        self.readers = {}
        self.waited = {s: {} for s in self.E}
        self.breq = {s: {} for s in self.E}
        self.n_ins = 0
        self.n_wait = 0

    def add(self, stream, fn, r=(), w=(), x=(), dkey=None):
        ins = Ins()
        ins.stream = stream
        ins.fn = fn
        ins.signal = False
        ins.is_dma = dkey is not None
        ins.dkey = dkey
        ins.sem = None
        ins.thresh = 0
        deps = {}

        def dep(d, raw):
            if d is None:
                return
            if d.stream == stream and not d.is_dma and not ins.is_dma:
                if stream == "pe" or not raw:
                    return
            deps[id(d)] = d

        for k in r:
            dep(self.lastw.get(k), True)
        for k in w:
            dep(self.lastw.get(k), False)
            for rd in self.readers.get(k, {}).values():
                dep(rd, False)
        for k in x:
            dep(self.lastw.get(k), False)
        rk = (stream, dkey)
        for k in r:
            self.readers.setdefault(k, {})[rk] = ins
        for k in w:
            self.lastw[k] = ins
            self.readers[k] = {}
        for k in x:
            self.lastw[k] = ins
        for d in deps.values():
            d.signal = True
        ins.deps = list(deps.values())
        self.pending.append(ins)
        return ins

    def _emit(self, ins):
        s = ins.stream
        eng = self.E[s]
        wl = self.waited[s]
        need = self.breq[s]
        self.breq[s] = {}
        for d in ins.deps:
            key = d.sem.name
            if need.get(key, (None, 0))[1] < d.thresh:
                need[key] = (d.sem, d.thresh)
        for key, (sem, val) in need.items():
            if wl.get(key, 0) < val:
                eng.wait_ge(sem, val)
                wl[key] = val
                self.n_wait += 1
        bi = ins.fn(eng)
        self.n_ins += 1
        if ins.is_dma:
            ent = self.dsem.get(ins.dkey)
            if ent is None:
                ent = [self.nc.alloc_semaphore("d_" + ins.dkey), 0]
                self.dsem[ins.dkey] = ent
            ent[1] += 16
            bi.then_inc(ent[0], 16)
            ins.sem = ent[0]
            ins.thresh = ent[1]
        elif ins.signal:
            self.ccnt[s] += 1
            bi.then_inc(self.csem[s], 1)
            ins.sem = self.csem[s]
            ins.thresh = self.ccnt[s]

    def flush(self, barrier=True):
        if barrier:
            last = {}
            for ins in self.pending:
                if not ins.is_dma:
                    last[ins.stream] = ins
            for ins in last.values():
                ins.signal = True
        for ins in self.pending:
            self._emit(ins)
        self.pending = []
        if barrier:
            for s in self.E:
                req = self.breq[s]
                for t, sem in self.csem.items():
                    if self.ccnt[t] > 0:
                        req[sem.name] = (sem, self.ccnt[t])
                for ent in self.dsem.values():
                    req[ent[0].name] = (ent[0], ent[1])
            self.lastw = {}
            self.readers = {}

    def finish(self):
        self.flush(barrier=True)
        for s in self.E:
            eng = self.E[s]
            for key, (sem, val) in self.breq[s].items():
                if self.waited[s].get(key, 0) < val:
                    eng.wait_ge(sem, val)
                    self.waited[s][key] = val
            self.breq[s] = {}


class Mem:
    def __init__(self, nc):
        self.nc = nc
        self.base = (nc.sbuf_base + 63) // 64 * 64
        self.top = nc.sbuf_top
        self.cur = self.base
        self.n = 0

    def alloc(self, name, shape, dtype):
        nbytes = int(np.prod(shape[1:])) * (4 if dtype == F32 else 2)
        nbytes = (nbytes + 63) // 64 * 64
        off = self.cur
        assert off + nbytes <= self.top, (name, off, nbytes, self.top)
        self.cur += nbytes
        self.n += 1
        return self.nc.alloc_sbuf_tensor_at(f"{name}_{self.n}", list(shape), dtype, offset=off)

    def mark(self):
        return self.cur

    def reset(self, m):
        self.cur = m


def build_program():
    nc = bass.Bass("TRN2", target_bir_lowering=False)
    S = Sched(nc)
    M = Mem(nc)

    def din(name, shape, dt):
        return nc.dram_tensor(name, list(shape), dt, kind="ExternalInput").ap()

    def dscr(name, shape, dt):
        kind = "ExternalOutput" if DEBUG else "Internal"
        return nc.dram_tensor(name, list(shape), dt, kind=kind).ap()

    x_d = din("x", [T, D], F32)
    xo_d = din("xo", [TO, D], F32)
    win_d = din("w_in", [D, NCOL], F32)
    wout_d = din("w_out", [D, D], F32)
    cvec_d = din("cvec", [128, CV], F32)
    cmat_d = din("cmat", [128, CM], BF16)
    out_d = nc.dram_tensor("out", [TO, D], F32, kind="ExternalOutput").ap()

    kT_d = dscr("s_kT", [8, 128, T], BF16)
    vF_d = dscr("s_vF", [T, 1024], BF16)
    qT_d = dscr("s_qT", [8, 128, TO], BF16)
    gz_d = dscr("s_gz", [TO, 1024], BF16)
    qmT_d = dscr("s_qmT", [4, 128, T], BF16)
    kmT_d = dscr("s_kmT", [4, 128, T], BF16)
    vM_d = dscr("s_vM", [T, 1024], BF16)
    go_d = dscr("s_go", [TO, 1024], BF16)
    gmz_d = dscr("s_gmz", [TO, 1024], BF16)
    if DEBUG:
        dbg_g = nc.dram_tensor("dbg_g", [128, 16, 768], F32, kind="ExternalOutput").ap()
        dbg_y = nc.dram_tensor("dbg_yT", [128, 16, TO], BF16, kind="ExternalOutput").ap()

    cvec = M.alloc("cvec", [128, CV], F32)
    cmat = M.alloc("cmat", [128, CM], BF16)
    graw = M.alloc("graw", [128, 32, 24], F32)
    ident = cmat[:, M_ID:M_ID + 128]
    cst = M.alloc("cst", [128, 8], F32)
    epsc = cst[:, 0:1]
    p_mark = M.mark()

    psb = [nc.alloc_psum_tensor(f"psb{i}", [128, 512], F32) for i in range(8)]

    def PK(i):
        return ("ps", i)

    S.add("sp", lambda e: e.dma_start(out=cvec[:, :], in_=cvec_d[:, :]), w=["cvec"], dkey="cvec")
    S.add("sp", lambda e: e.dma_start(out=cmat[:, :], in_=cmat_d[:, :]), w=["cmat"], dkey="cmat")
    S.add("dve", lambda e: e.memset(cst[:, 0:1], EPS), w=["cst"])
    S.add("dve", lambda e: e.memset(cst[:, 1:2], 1.0), w=["cst"])
    S.add("dve", lambda e: e.memset(cst[:, 2:3], -LN8), w=["cst"])

    hT = M.alloc("hT", [128, KC, 2048], BF16)
    wb = [M.alloc(f"wb{i}", [128, KC, 512], BF16) for i in range(2)]
    xt = [M.alloc(f"xt{i}", [128, D], F32) for i in range(2)]
    hn = [M.alloc(f"hn{i}", [128, D], BF16) for i in range(2)]
    junk = M.alloc("junk", [128, D], BF16)
    st = [M.alloc(f"st{i}", [128, 512], BF16) for i in range(4)]
    ub = [M.alloc(f"ub{i}", [128, 515], F32) for i in range(8)]
    cacc = [M.alloc(f"cacc{i}", [128, 512], F32) for i in range(2)]
    sm = M.alloc("sm", [128, 64], F32)
    pst_bf = [psb[6].bitcast(BF16), psb[7].bitcast(BF16)]

    for i in range(8):
        S.add("dve", (lambda i: lambda e: e.memset(ub[i][:, 0:3], 0.0))(i), w=[("ub", i)])

    st_ctr = [0]
    ps_ctr = [0]
    sm_ctr = [0]

    def next_st():
        i = st_ctr[0] % 4
        st_ctr[0] += 1
        return i

    def next_ps():
        i = ps_ctr[0] % 6
        ps_ctr[0] += 1
        return i

    def build_hT(xsrc, ntb=16):
        for tb in range(ntb):
            b = tb % 2
            S.add("sp", (lambda b, tb: lambda e: e.dma_start(out=xt[b][:, :], in_=xsrc[tb * 128:(tb + 1) * 128, :]))(b, tb),
                  w=[("xt", b)], dkey=f"xt{b}")
            so = (sm_ctr[0] % 8) * 4
            sm_ctr[0] += 1
            ss = sm[:, so:so + 1]
            rs = sm[:, so + 1:so + 2]
            S.add("act", (lambda b, ss: lambda e: e.activation(out=junk[:, :], in_=xt[b][:, :], func=AF.Square, accum_out=ss))(b, ss),
                  r=[("xt", b)], w=["junk", ("sm", so)])
            S.add("act", (lambda ss, rs: lambda e: e.activation(out=rs, in_=ss, func=AF.Sqrt, scale=1.0 / D, bias=epsc))(ss, rs),
                  r=[("sm", so), "cst"], w=[("sm", so + 1)])
            S.add("dve", (lambda rs: lambda e: e.reciprocal(out=rs, in_=rs))(rs),
                  r=[("sm", so + 1)], w=[("sm", so + 1)])
            S.add("dve", (lambda b, rs: lambda e: e.scalar_tensor_tensor(out=hn[b][:, :], in0=xt[b][:, :], scalar=rs, in1=cvec[:, V_GN:V_GN + D], op0=ALU.mult, op1=ALU.mult))(b, rs),
                  r=[("xt", b), ("sm", so + 1), "cvec"], w=[("hn", b)])
            for half in range(2):
                pb = 6 + half
                for j in range(8):
                    kc = half * 8 + j
                    S.add("pe", (lambda b, kc, j, half: lambda e: e.transpose(out=pst_bf[half][:, j * 128:(j + 1) * 128], in_=hn[b][:, kc * 128:(kc + 1) * 128], identity=ident))(b, kc, j, half),
                          r=[("hn", b), "cmat"], x=[PK(pb)])
                dst = hT[:, half * 8:(half + 1) * 8, tb * 128:(tb + 1) * 128]
                src = pst_bf[half][:, :].rearrange("p (j t) -> p j t", j=8)
                if half == 0:
                    S.add("act", (lambda dst, src: lambda e: e.copy(out=dst, in_=src))(dst, src), x=[PK(pb)], w=[("hT", tb, half)])
                else:
                    S.add("dve", (lambda dst, src: lambda e: e.tensor_copy(out=dst, in_=src))(dst, src), x=[PK(pb)], w=[("hT", tb, half)])

    def hT_keys(tbs):
        return [("hT", tb, h) for tb in tbs for h in range(2)]

    def load_w(g, cols):
        b = g % 2
        o = 0
        for (c0, n) in cols:
            src = win_d[:, c0:c0 + n].rearrange("(kc p) n -> p kc n", p=128)
            S.add("pool", (lambda b, o, n, src: lambda e: e.dma_start(out=wb[b][:, :, o:o + n], in_=src))(b, o, n, src),
                  w=[("wb", b)], dkey=f"wb{b}")
            o += n
        return b

    def mm_tok(b, tb, ncols):
        pi = next_ps()
        for kc in range(KC):
            S.add("pe", (lambda pi, kc, tb, b, ncols: lambda e: e.matmul(psb[pi][:, 0:ncols], lhsT=hT[:, kc, tb * 128:(tb + 1) * 128], rhs=wb[b][:, kc, 0:ncols], start=(kc == 0), stop=(kc == KC - 1)))(pi, kc, tb, b, ncols),
                  r=hT_keys([tb]) + [("wb", b)], x=[PK(pi)])
        return pi

    def mm_feat(b, ct, sb):
        pi = next_ps()
        for kc in range(KC):
            S.add("pe", (lambda pi, kc, ct, sb, b: lambda e: e.matmul(psb[pi][:, :], lhsT=wb[b][:, kc, ct * 128:(ct + 1) * 128], rhs=hT[:, kc, sb * 512:(sb + 1) * 512], start=(kc == 0), stop=(kc == KC - 1)))(pi, kc, ct, sb, b),
                  r=hT_keys(range(sb * 4, sb * 4 + 4)) + [("wb", b)], x=[PK(pi)])
        return pi

    def evac_store(pi, dst_ap, func=None, eng="dve"):
        si = next_st()
        if func is None:
            if eng == "dve":
                S.add("dve", (lambda pi, si: lambda e: e.tensor_copy(out=st[si][:, :], in_=psb[pi][:, :]))(pi, si), x=[PK(pi)], w=[("st", si)])
            else:
                S.add("act", (lambda pi, si: lambda e: e.copy(out=st[si][:, :], in_=psb[pi][:, :]))(pi, si), x=[PK(pi)], w=[("st", si)])
        else:
            S.add("act", (lambda pi, si, func: lambda e: e.activation(out=st[si][:, :], in_=psb[pi][:, :], func=func))(pi, si, func), x=[PK(pi)], w=[("st", si)])
        S.add("sp", (lambda si, dst_ap: lambda e: e.dma_start(out=dst_ap, in_=st[si][:, :]))(si, dst_ap), r=[("st", si)], dkey=f"st{si}")

    def conv_evac(pi, ui, dst_ap, parity):
        u = ub[ui]
        ca = cacc[parity]
        cw = lambda j: cvec[:, V_CW + ui * 4 + j:V_CW + ui * 4 + j + 1]
        cb = cvec[:, V_CB + ui:V_CB + ui + 1]
        S.add("act", (lambda pi, u: lambda e: e.copy(out=u[:, 3:515], in_=psb[pi][:, :]))(pi, u), x=[PK(pi)], w=[("ubm", ui)])
        S.add("dve", (lambda u, ca: lambda e: e.tensor_scalar(out=ca[:, :], in0=u[:, 3:515], scalar1=cw(3), scalar2=cb, op0=ALU.mult, op1=ALU.add))(u, ca),
              r=[("ubm", ui), "cvec"], w=[("cacc", parity)])
        for j in (2, 1, 0):
            S.add("dve", (lambda u, ca, j: lambda e: e.scalar_tensor_tensor(out=ca[:, :], in0=u[:, j:j + 512], scalar=cw(j), in1=ca[:, :], op0=ALU.mult, op1=ALU.add))(u, ca, j),
                  r=[("ubm", ui), ("ub", ui), ("cacc", parity)], w=[("cacc", parity)])
        S.add("dve", (lambda u: lambda e: e.tensor_copy(out=u[:, 0:3], in_=u[:, 512:515]))(u), r=[("ubm", ui)], w=[("ub", ui)])
        si = next_st()
        S.add("act", (lambda ca, si: lambda e: e.activation(out=st[si][:, :], in_=ca[:, :], func=AF.Silu))(ca, si), r=[("cacc", parity)], w=[("st", si)])
        S.add("sp", (lambda si, dst_ap: lambda e: e.dma_start(out=dst_ap, in_=st[si][:, :]))(si, dst_ap), r=[("st", si)], dkey=f"st{si}")

    gctr = [0]

    def full_pass(ps_id):
        tok0 = ps_id * 2048
        build_hT(x_d[tok0:tok0 + 2048, :])
        b = load_w(gctr[0], [(C_FF, 8), (C_MI, 16)])
        gctr[0] += 1
        for tb in range(16):
            pi = mm_tok(b, tb, 24)
            gtb = ps_id * 16 + tb
            S.add("dve", (lambda pi, gtb: lambda e: e.tensor_copy(out=graw[:, gtb, :], in_=psb[pi][:, 0:24]))(pi, gtb), x=[PK(pi)], w=[("graw", gtb)])
        for gi in range(2):
            b = load_w(gctr[0], [(C_FK + gi * 512, 512)])
            gctr[0] += 1
            for ct in range(4):
                h = gi * 4 + ct
                for sb in range(4):
                    pi = mm_feat(b, ct, sb)
                    evac_store(pi, kT_d[h, :, tok0 + sb * 512:tok0 + (sb + 1) * 512], eng=("dve" if sb % 2 else "act"))
        for gi, (c0, dd) in enumerate(((C_MQ, qmT_d), (C_MK, kmT_d))):
            b = load_w(gctr[0], [(c0, 512)])
            gctr[0] += 1
            for ct in range(4):
                for sb in range(4):
                    pi = mm_feat(b, ct, sb)
                    conv_evac(pi, gi * 4 + ct, dd[ct, :, tok0 + sb * 512:tok0 + (sb + 1) * 512], sb % 2)
        for (c0, dd) in ((C_FV, vF_d), (C_MV, vM_d)):
            for gi in range(2):
                b = load_w(gctr[0], [(c0 + gi * 512, 512)])
                gctr[0] += 1
                for tb in range(16):
                    pi = mm_tok(b, tb, 512)
                    evac_store(pi, dd[tok0 + tb * 128:tok0 + (tb + 1) * 128, gi * 512:(gi + 1) * 512], eng=("dve" if tb % 2 else "act"))

    def own_pass():
        build_hT(xo_d[:, :])
        for gi in range(2):
            b = load_w(gctr[0], [(C_FQ + gi * 512, 512)])
            gctr[0] += 1
            for ct in range(4):
                h = gi * 4 + ct
                for sb in range(4):
                    pi = mm_feat(b, ct, sb)
                    evac_store(pi, qT_d[h, :, sb * 512:(sb + 1) * 512], eng="dve")
        for (c0, dd, fn) in ((C_FZ, gz_d, AF.Silu), (C_MO, go_d, AF.Sigmoid), (C_MZ, gmz_d, AF.Silu)):
            for gi in range(2):
                b = load_w(gctr[0], [(c0 + gi * 512, 512)])
                gctr[0] += 1
                for tb in range(16):
                    pi = mm_tok(b, tb, 512)
                    evac_store(pi, dd[tb * 128:(tb + 1) * 128, gi * 512:(gi + 1) * 512], func=fn)

    full_pass(0)
    full_pass(1)
    own_pass()
    S.flush(barrier=True)
    M.reset(p_mark)

    yT_off = (M.top - KC * TO * 2) // 64 * 64
    yT = nc.alloc_sbuf_tensor_at("yT", [128, KC, TO], BF16, offset=yT_off)
    M.top = yT_off
    mask = cmat[:, M_MASK:M_MASK + 4096].rearrange("p (r t) -> p r t", r=8)
    sel0 = cvec[:, V_SEL:V_SEL + 1]
    sel1 = cvec[:, V_SEL + 1:V_SEL + 2]
    onec = cst[:, 1:2]
    nln8 = cst[:, 2:3]
    pstb = psb[7].bitcast(BF16)

    zg = M.alloc("zg", [128, 32, 24], F32)
    el = M.alloc("el", [128, 32, 24], F32)
    cum = M.alloc("cum", [128, 32, 24], F32)
    wi = M.alloc("wi", [128, 32, 24], F32)
    G = M.alloc("G", [128, 33, 24], F32)
    biasF = M.alloc("biasF", [128, 4, 32, 8], F32)
    refF = M.alloc("refF", [128, 4, 8], F32)
    t8 = M.alloc("t8", [128, 4, 8], F32)
    uu = M.alloc("uu", [128, 32, 8], F32)
    EX = M.alloc("EX", [128, 832], F32)
    wv = EX[:, 0:256].rearrange("p (a b) -> p a b", a=32)
    wh = EX[:, 256:512].rearrange("p (a b) -> p a b", a=32)
    eb = EX[:, 512:768].rearrange("p (a b) -> p a b", a=32)
    dC = EX[:, 768:800].rearrange("p (a b) -> p a b", a=4)
    dH = EX[:, 800:832].rearrange("p (a b) -> p a b", a=4)
    ebo = M.alloc("ebo", [128, 4, 4, 8], F32)
    ebt = M.alloc("ebt", [128, 4, 4, 8], F32)
    g_mark = M.mark()

    flat = lambda t: t[:, :, :].rearrange("p a b -> p (a b)")
    S.add("dve", lambda e: e.tensor_tensor(out=flat(zg), in0=flat(graw), in1=cvec[:, V_GB:V_GB + 768], op=ALU.add),
          r=["cvec"] + [("graw", i) for i in range(32)], w=["zg"])
    S.add("act", lambda e: e.activation(out=flat(el), in_=flat(zg), func=AF.Exp, scale=-1.0), r=["zg"], w=["el"])
    S.add("act", lambda e: e.activation(out=flat(el), in_=flat(el), func=AF.Ln, bias=onec, scale=1.0), r=["el", "cst"], w=["el"])
    tri = cvec[:, V_TRI:V_TRI + 128]
    ones = cvec[:, V_ONE:V_ONE + 128]
    for half in range(2):
        S.add("pe", (lambda half: lambda e: e.matmul(psb[half][:, 0:384], lhsT=ones, rhs=flat(el)[:, half * 384:(half + 1) * 384], start=True, stop=True))(half),
              r=["el", "cvec"], x=[PK(half)])
        S.add("pe", (lambda half: lambda e: e.matmul(psb[2 + half][:, 0:384], lhsT=tri, rhs=flat(el)[:, half * 384:(half + 1) * 384], start=True, stop=True))(half),
              r=["el", "cvec"], x=[PK(2 + half)])
        S.add("dve", (lambda half: lambda e: e.tensor_copy(out=flat(wi)[:, half * 384:(half + 1) * 384], in_=psb[half][:, 0:384]))(half), x=[PK(half)], w=[("wi", half)])
        S.add("dve", (lambda half: lambda e: e.tensor_copy(out=flat(cum)[:, half * 384:(half + 1) * 384], in_=psb[2 + half][:, 0:384]))(half), x=[PK(2 + half)], w=[("cumh", half)])
    S.add("dve", lambda e: e.memset(G[:, 0, :], 0.0), w=[("G", 0)])
    for kb in range(1, 33):
        S.add("dve", (lambda kb: lambda e: e.tensor_tensor(out=G[:, kb, :], in0=G[:, kb - 1, :], in1=wi[:, kb - 1, :], op=ALU.add))(kb),
              r=[("G", kb - 1), ("wi", (kb - 1) // 16)], w=[("G", kb)])
    S.add("dve", lambda e: e.tensor_tensor(out=cum[:, :, :], in0=cum[:, :, :], in1=G[:, 0:32, :], op=ALU.add),
          r=[("cumh", 0), ("cumh", 1)] + [("G", k) for k in range(33)], w=["cum"])
    for m in range(4):
        S.add("dve", (lambda m: lambda e: e.tensor_scalar(out=t8[:, m, :], in0=G[:, 8 * m + 2, 0:8], scalar1=sel0, scalar2=None, op0=ALU.mult))(m),
              r=["cum", "cvec"], w=[("t8", m)])
        S.add("dve", (lambda m: lambda e: e.scalar_tensor_tensor(out=refF[:, m, :], in0=G[:, 8 * m + 6, 0:8], scalar=sel1, in1=t8[:, m, :], op0=ALU.mult, op1=ALU.add))(m),
              r=["cum", "cvec", ("t8", m)], w=[("refF", m)])
        for kb in range(8 * m + 8):
            if kb < 8 * m:
                S.add("dve", (lambda m, kb: lambda e: e.tensor_tensor(out=biasF[:, m, kb, :], in0=cum[:, kb, 0:8], in1=refF[:, m, :], op=ALU.subtract))(m, kb),
                      r=["cum", ("refF", m)], w=["biasF"])
            else:
                kl = cvec[:, V_KILL + kb - 8 * m:V_KILL + kb - 8 * m + 1]
                S.add("dve", (lambda m, kb, kl: lambda e: e.scalar_tensor_tensor(out=biasF[:, m, kb, :], in0=cum[:, kb, 0:8], scalar=kl, in1=refF[:, m, :], op0=ALU.add, op1=ALU.subtract))(m, kb, kl),
                      r=["cum", ("refF", m), "cvec"], w=["biasF"])
    S.add("dve", lambda e: e.tensor_tensor(out=uu[:, :, :], in0=zg[:, :, 8:16], in1=cum[:, :, 16:24], op=ALU.add), r=["zg", "cum"], w=["uu"])
    for m in range(4):
        for kb in range(8 * m, 8 * m + 8):
            S.add("dve", (lambda m, kb: lambda e: e.tensor_tensor(out=wv[:, kb, :], in0=uu[:, kb, :], in1=G[:, 8 * m + 4, 16:24], op=ALU.subtract))(m, kb),
                  r=["uu", "cum"], w=["EX"])
            S.add("dve", (lambda m, kb: lambda e: e.tensor_tensor(out=wh[:, kb, :], in0=uu[:, kb, :], in1=G[:, 8 * m + 8, 16:24], op=ALU.subtract))(m, kb),
                  r=["uu", "cum"], w=["EX"])
            S.add("dve", (lambda m, kb: lambda e: e.tensor_tensor(out=eb[:, kb, :], in0=G[:, 8 * m + 4, 16:24], in1=cum[:, kb, 16:24], op=ALU.subtract))(m, kb),
                  r=["cum"], w=["EX"])
        S.add("dve", (lambda m: lambda e: e.tensor_tensor(out=dC[:, m, :], in0=G[:, 8 * m, 16:24], in1=G[:, 8 * m + 8, 16:24], op=ALU.subtract))(m), r=["cum"], w=["EX"])
        S.add("dve", (lambda m: lambda e: e.tensor_tensor(out=dH[:, m, :], in0=G[:, 8 * m, 16:24], in1=G[:, 8 * m + 4, 16:24], op=ALU.subtract))(m), r=["cum"], w=["EX"])
    S.add("act", lambda e: e.activation(out=EX[:, 0:512], in_=EX[:, 0:512], func=AF.Exp, bias=nln8, scale=1.0), r=["EX", "cst"], w=["EX"])
    S.add("act", lambda e: e.activation(out=EX[:, 512:832], in_=EX[:, 512:832], func=AF.Exp), r=["EX"], w=["EX"])
    for m in range(4):
        S.add("dve", (lambda m: lambda e: e.tensor_scalar(out=ebt[:, m, :, :], in0=eb[:, 8 * m:8 * m + 4, :], scalar1=sel0, scalar2=None, op0=ALU.mult))(m),
              r=["EX", "cvec"], w=[("ebt", m)])
        S.add("dve", (lambda m: lambda e: e.scalar_tensor_tensor(out=ebo[:, m, :, :], in0=eb[:, 8 * m + 4:8 * m + 8, :], scalar=sel1, in1=ebt[:, m, :, :], op0=ALU.mult, op1=ALU.add))(m),
              r=["EX", "cvec", ("ebt", m)], w=["ebo"])
    S.flush(barrier=True)

    if DEBUG:
        dl = [(0, graw), (1, zg), (2, el), (3, cum), (4, wi)]
        for i, t in dl:
            S.add("sp", (lambda i, t: lambda e: e.dma_start(out=dbg_g[:, i, :], in_=flat(t)))(i, t), dkey="dbg")
        S.add("sp", lambda e: e.dma_start(out=dbg_g[:, 5, :], in_=G[:, 0:32, :].rearrange("p a b -> p (a b)")), dkey="dbg")
        S.add("sp", lambda e: e.dma_start(out=dbg_g[:, 6, :], in_=EX[:, 0:768]), dkey="dbg")
        S.add("sp", lambda e: e.dma_start(out=dbg_g[:, 7, 0:64], in_=EX[:, 768:832]), dkey="dbg")
        S.add("sp", lambda e: e.dma_start(out=dbg_g[:, 7, 64:192], in_=ebo[:, :, :, :].rearrange("p a b c -> p (a b c)")), dkey="dbg")
        for m in range(4):
            S.add("sp", (lambda m: lambda e: e.dma_start(out=dbg_g[:, 8 + m, 0:256], in_=biasF[:, m, :, :].rearrange("p a b -> p (a b)")))(m), dkey="dbg")
        S.flush(barrier=True)
    if STOP_AFTER in ("A", "B"):
        S.finish()
        return nc, S

    def MM(out, lhsT, rhs, start, stop, r, x):
        S.add("pe", lambda e: e.matmul(out, lhsT=lhsT, rhs=rhs, start=start, stop=stop), r=r, x=x)

    def TR(out, in_, r, x):
        S.add("pe", lambda e: e.transpose(out=out, in_=in_, identity=ident), r=list(r) + ["cmat"], x=x)

    def ACT(out, in_, func, r=(), w=(), x=(), **kw):
        S.add("act", lambda e: e.activation(out=out, in_=in_, func=func, **kw), r=r, w=w, x=x)

    def TS(eng, out, in0, s1, s2, op0, op1, r=(), w=(), x=()):
        if op1 is None:
            S.add(eng, lambda e: e.tensor_scalar(out=out, in0=in0, scalar1=s1, scalar2=None, op0=op0), r=r, w=w, x=x)
        else:
            S.add(eng, lambda e: e.tensor_scalar(out=out, in0=in0, scalar1=s1, scalar2=s2, op0=op0, op1=op1), r=r, w=w, x=x)

    def STT(eng, out, in0, scalar, in1, op0, op1, r=(), w=(), x=()):
        S.add(eng, lambda e: e.scalar_tensor_tensor(out=out, in0=in0, scalar=scalar, in1=in1, op0=op0, op1=op1), r=r, w=w, x=x)

    def TT(eng, out, in0, in1, op, r=(), w=(), x=()):
        S.add(eng, lambda e: e.tensor_tensor(out=out, in0=in0, in1=in1, op=op), r=r, w=w, x=x)

    def CP(eng, out, in_, r=(), w=(), x=()):
        S.add(eng, lambda e: e.tensor_copy(out=out, in_=in_), r=r, w=w, x=x)

    def RCP(out, in_, r=(), w=(), x=()):
        S.add("dve", lambda e: e.reciprocal(out=out, in_=in_), r=r, w=w, x=x)

    def DMA(eng, out, in_, dkey, r=(), w=()):
        S.add(eng, lambda e: e.dma_start(out=out, in_=in_), r=r, w=w, dkey=dkey)

    def MS(eng, ap, val, w):
        S.add(eng, lambda e: e.memset(ap, val), w=w)

    SCALE = 128.0 ** -0.5
    kTh = [M.alloc(f"kTh{i}", [128, T], BF16) for i in range(2)]
    vh = [M.alloc(f"vh{i}", [128, 32, 132], BF16) for i in range(2)]
    qTh = [M.alloc(f"qTh{i}", [128, TO], BF16) for i in range(2)]
    gzh = [M.alloc(f"gzh{i}", [128, 16, 128], BF16) for i in range(2)]
    pT = [M.alloc(f"pT{i}", [128, 512], BF16) for i in range(4)]
    ctr = dict(pT=0, t1=0, yb=0, sm=0, hb=0)

    def rot(name, n):
        i = ctr[name] % n
        ctr[name] += 1
        return i

    def alloc_small(tag):
        d = {}
        d["t1"] = [M.alloc(f"t1{tag}{i}", [128, 128], F32) for i in range(2)]
        d["hb"] = [M.alloc(f"hb{tag}{i}", [128, 128], F32) for i in range(2)]
        d["yb"] = [M.alloc(f"yb{tag}{i}", [128, 128], BF16) for i in range(8)]
        d["junk"] = M.alloc(f"junk2{tag}", [128, 128], F32)
        d["sm"] = M.alloc(f"sm2{tag}", [128, 64], F32)
        return d

    def head_norm_out(sb, src, src_r, src_x, sc_ap, sc_keys, gate_t, gate_key, gvec_off, c, kc_out, m):
        sm2 = sb["sm"]
        so = rot("sm", 8) * 8
        ssq = sm2[:, so:so + 1]
        r1 = sm2[:, so + 1:so + 2]
        rr = sm2[:, so + 2:so + 3]
        ACT(sb["junk"][:, :], src, AF.Square, r=list(sc_keys) + list(src_r), x=src_x, w=["junk2", ("sm2", so)], scale=sc_ap, accum_out=ssq)
        ACT(r1, ssq, AF.Sqrt, r=[("sm2", so), "cst"], w=[("sm2", so + 1)], scale=1.0 / 128, bias=epsc)
        RCP(r1, r1, r=[("sm2", so + 1)], w=[("sm2", so + 1)])
        TT("dve", rr, r1, sc_ap, ALU.mult, r=[("sm2", so + 1)] + list(sc_keys), w=[("sm2", so + 2)])
        ti = rot("t1", 2)
        STT("dve", sb["t1"][ti][:, :], src, rr, cvec[:, gvec_off:gvec_off + 128], ALU.mult, ALU.mult,
            r=[("sm2", so + 2), "cvec"] + list(src_r), x=src_x, w=[("t1", ti)])
        yi = rot("yb", 8)
        TT("pool", sb["yb"][yi][:, :], sb["t1"][ti][:, :], gate_t, ALU.mult, r=[("t1", ti), gate_key], w=[("yb", yi)])
        TR(pstb[:, c * 128:(c + 1) * 128], sb["yb"][yi][:, :], r=[("yb", yi)], x=[PK(7)])
        if c == 3:
            CP("dve", yT[:, kc_out, m * 512:(m + 1) * 512], pstb[:, 0:512], x=[PK(7)], w=[("yT", kc_out, m)])

    sbD = alloc_small("d")
    for b in range(2):
        MS("pool", vh[b][:, :, 128:129], 1.0, w=[("vh1", b)])

    def fox_head(h):
        b = h % 2
        DMA("sp", kTh[b][:, :], kT_d[h, :, :], f"kTh{b}", w=[("kTh", b)])
        DMA("sp", qTh[b][:, :], qT_d[h, :, :], f"qTh{b}", w=[("qTh", b)])
        DMA("sp", vh[b][:, :, 0:128], vF_d[:, h * 128:(h + 1) * 128].rearrange("(kb p) d -> p kb d", p=128), f"vh{b}", w=[("vh", b)])
        DMA("sp", gzh[b][:, :, :], gz_d[:, h * 128:(h + 1) * 128].rearrange("(tb p) d -> p tb d", p=128), f"gzh{b}", w=[("gzh", b)])
        for m in range(4):
            nkb = 8 * m + 8

            def emit_S(kb):
                pi = kb % 3
                MM(psb[pi][:, :], kTh[b][:, kb * 128:(kb + 1) * 128], qTh[b][:, m * 512:(m + 1) * 512], True, True,
                   r=[("kTh", b), ("qTh", b)], x=[PK(pi)])

            emit_S(0)
            emit_S(1)
            for kb in range(nkb):
                if kb + 2 < nkb:
                    emit_S(kb + 2)
                pi = kb % 3
                pj = rot("pT", 4)
                ACT(pT[pj][:, :], psb[pi][:, :], AF.Exp, r=["biasF"], x=[PK(pi)], w=[("pT", pj)],
                    bias=biasF[:, m, kb, h:h + 1], scale=SCALE)
                if kb >= 8 * m:
                    TT("pool", pT[pj][:, :], pT[pj][:, :], mask[:, kb - 8 * m, :], ALU.mult, r=[("pT", pj), "cmat"], w=[("pT", pj)])
                for c in range(4):
                    MM(psb[3 + c][:, 0:129], pT[pj][:, c * 128:(c + 1) * 128], vh[b][:, kb, 0:129], kb == 0, kb == nkb - 1,
                       r=[("pT", pj), ("vh", b), ("vh1", b)], x=[PK(3 + c)])
            for c in range(4):
                so = rot("sm", 8) * 8
                rl = sbD["sm"][:, so + 4:so + 5]
                RCP(rl, psb[3 + c][:, 128:129], x=[PK(3 + c)], w=[("sm2", so + 4)])
                head_norm_out(sbD, psb[3 + c][:, 0:128], [], [PK(3 + c)], rl, [("sm2", so + 4)],
                              gzh[b][:, 4 * m + c, :], ("gzh", b), V_GFOX + h * 128, c, h, m)

    for h in range(8):
        fox_head(h)
    S.flush(barrier=True)
    M.reset(g_mark)
    if DEBUG and STOP_AFTER == "D":
        DMA("sp", dbg_y[:, :, :], yT[:, :, :], "dbg")
        S.flush(barrier=True)
    if STOP_AFTER == "D":
        S.finish()
        return nc, S

    kmT = M.alloc("kmT", [128, T], BF16)
    qmT = M.alloc("qmT", [128, T], BF16)
    qown = M.alloc("qown", [128, TO], BF16)
    qtmp = M.alloc("qtmp", [128, 512], BF16)
    Kh = M.alloc("Kh", [128, 32, 128], BF16)
    vmh = [M.alloc(f"vmh{i}", [128, 32, 132], BF16) for i in range(2)]
    goh = [M.alloc(f"goh{i}", [128, 16, 128], BF16) for i in range(2)]
    gmzh = [M.alloc(f"gmzh{i}", [128, 16, 128], BF16) for i in range(2)]
    aT = [M.alloc(f"aT{i}", [128, 512], BF16) for i in range(4)]
    Cst = M.alloc("Cst", [128, 132], F32)
    Ct = M.alloc("Ct", [128, 132], BF16)
    sbE = alloc_small("e")
    for b in range(2):
        MS("pool", vmh[b][:, :, 128:129], 1.0, w=[("vmh1", b)])

    def mlstm_ct(ct):
        DMA("sp", kmT[:, :], kmT_d[ct, :, :], "kmT", w=["kmT"])
        DMA("sp", qmT[:, :], qmT_d[ct, :, :], "qmT", w=["qmT"])
        for m in range(4):
            TS("pool", qown[:, m * 512:(m + 1) * 512], qmT[:, 1024 * m:1024 * m + 512], sel0, None, ALU.mult, None,
               r=["qmT", "cvec"], w=[("qown", m)])
            TS("pool", qtmp[:, :], qmT[:, 1024 * m + 512:1024 * m + 1024], sel1, None, ALU.mult, None,
               r=["qmT", "cvec"], w=["qtmp"])
            TT("pool", qown[:, m * 512:(m + 1) * 512], qown[:, m * 512:(m + 1) * 512], qtmp[:, :], ALU.add,
               r=["qtmp", ("qown", m)], w=[("qown", m)])
        for t0 in range(0, 32, 8):
            for j in range(8):
                tb = t0 + j
                TR(pstb[:, j * 128:(j + 1) * 128], kmT[:, tb * 128:(tb + 1) * 128], r=["kmT"], x=[PK(7)])
            for j in range(8):
                tb = t0 + j
                for hh in range(2):
                    h = 2 * ct + hh
                    TS("dve", Kh[:, tb, hh * 64:(hh + 1) * 64], pstb[:, j * 128 + hh * 64:j * 128 + (hh + 1) * 64], wh[:, tb, h:h + 1], None, ALU.mult, None,
                       r=["EXr"], x=[PK(7)], w=[("Kh", tb)])
        for hh in range(2):
            h = 2 * ct + hh
            b = h % 2
            po = 64 * hh
            DMA("sp", vmh[b][:, :, 0:128], vM_d[:, h * 128:(h + 1) * 128].rearrange("(kb p) d -> p kb d", p=128), f"vmh{b}", w=[("vmh", b)])
            DMA("sp", goh[b][:, :, :], go_d[:, h * 128:(h + 1) * 128].rearrange("(tb p) d -> p tb d", p=128), f"goh{b}", w=[("goh", b)])
            DMA("sp", gmzh[b][:, :, :], gmz_d[:, h * 128:(h + 1) * 128].rearrange("(tb p) d -> p tb d", p=128), f"gmzh{b}", w=[("gmzh", b)])
            MS("dve", Cst[:, :], 0.0, w=["Cst"])
            for m in range(4):
                TS("dve", Ct[po:po + 64, 0:129], Cst[po:po + 64, 0:129], dH[po:po + 64, m, h:h + 1], None, ALU.mult, None,
                   r=["Cst", "EXr"], w=["Ct"])
                for r_ in range(8):
                    kb = 8 * m + r_
                    pi = r_ % 2
                    MM(psb[pi][:, :], kmT[po:po + 64, kb * 128:(kb + 1) * 128], qown[po:po + 64, m * 512:(m + 1) * 512], True, True,
                       r=["kmT", ("qown", m)], x=[PK(pi)])
                    aj = rot("pT", 4)
                    STT("dve", aT[aj][:, :], psb[pi][:, :], wv[:, kb, h:h + 1], mask[:, r_, :], ALU.mult, ALU.mult,
                        r=["EXr", "cmat"], x=[PK(pi)], w=[("aT", aj)])
                    for c in range(4):
                        MM(psb[3 + c][:, 0:129], aT[aj][:, c * 128:(c + 1) * 128], vmh[b][:, kb, 0:129], r_ == 0, False,
                           r=[("aT", aj), ("vmh", b), ("vmh1", b)], x=[PK(3 + c)])
                for c in range(4):
                    MM(psb[3 + c][:, 0:129], qown[po:po + 64, m * 512 + c * 128:m * 512 + (c + 1) * 128], Ct[po:po + 64, 0:129], False, True,
                       r=[("qown", m), "Ct"], x=[PK(3 + c)])
                for c in range(4):
                    sm2 = sbE["sm"]
                    so = rot("sm", 8) * 8
                    den = sm2[:, so + 4:so + 5]
                    nd = sm2[:, so + 5:so + 6]
                    scl = sm2[:, so + 6:so + 7]
                    ebq = ebo[:, m, c, h:h + 1]
                    TT("dve", den, psb[3 + c][:, 128:129], ebq, ALU.mult, r=["EXr"], x=[PK(3 + c)], w=[("sm2", so + 4)])
                    TS("dve", nd, den, -1.0, None, ALU.mult, None, r=[("sm2", so + 4)], w=[("sm2", so + 5)])
                    TT("dve", nd, nd, den, ALU.max, r=[("sm2", so + 4), ("sm2", so + 5)], w=[("sm2", so + 5)])
                    TS("dve", nd, nd, 1.0, None, ALU.max, None, r=[("sm2", so + 5)], w=[("sm2", so + 5)])
                    RCP(nd, nd, r=[("sm2", so + 5)], w=[("sm2", so + 5)])
                    TT("dve", scl, nd, ebq, ALU.mult, r=[("sm2", so + 5), "EXr"], w=[("sm2", so + 6)])
                    hi = rot("hb", 2)
                    STT("dve", sbE["hb"][hi][:, :], psb[3 + c][:, 0:128], scl, goh[b][:, 4 * m + c, :], ALU.mult, ALU.mult,
                        r=[("sm2", so + 6), ("goh", b)], x=[PK(3 + c)], w=[("hb", hi)])
                    head_norm_out(sbE, sbE["hb"][hi][:, :], [("hb", hi)], [], onec, ["cst"],
                                  gmzh[b][:, 4 * m + c, :], ("gmzh", b), V_GML + h * 128, c, 8 + h, m)
                if m < 3:
                    for r_ in range(8):
                        kb = 8 * m + r_
                        MM(psb[2][:, 0:129], Kh[:, kb, :], vmh[b][:, kb, 0:129], r_ == 0, r_ == 7,
                           r=[("Kh", kb), ("vmh", b), ("vmh1", b)], x=[PK(2)])
                    STT("dve", Cst[po:po + 64, 0:129], Cst[po:po + 64, 0:129], dC[po:po + 64, m, h:h + 1], psb[2][po:po + 64, 0:129], ALU.mult, ALU.add,
                        r=["Cst", "EXr"], x=[PK(2)], w=["Cst"])

    for ct in range(4):
        mlstm_ct(ct)
    S.flush(barrier=True)
    M.reset(p_mark)
    if DEBUG and STOP_AFTER == "E":
        DMA("sp", dbg_y[:, :, :], yT[:, :, :], "dbg")
        S.flush(barrier=True)
    if STOP_AFTER == "E":
        S.finish()
        return nc, S

    wo = M.alloc("wo", [128, KC, D], BF16)
    xr = [M.alloc(f"xr{i}", [128, D], F32) for i in range(2)]
    junk3 = M.alloc("junk3", [128, D], BF16)
    sm3 = M.alloc("sm3", [128, 16], F32)
    for n in range(4):
        DMA("pool", wo[:, :, n * 512:(n + 1) * 512], wout_d[:, n * 512:(n + 1) * 512].rearrange("(kc p) n -> p kc n", p=128), f"wo{n}", w=[("wo", n)])
    pctr = 0
    for tb in range(16):
        b = tb % 2
        DMA("sp", xr[b][:, :], xo_d[tb * 128:(tb + 1) * 128, :], f"xr{b}", w=[("xr", b)])
        for n in range(4):
            pi = pctr % 6
            pctr += 1
            for kc in range(KC):
                MM(psb[pi][:, :], yT[:, kc, tb * 128:(tb + 1) * 128], wo[:, kc, n * 512:(n + 1) * 512], kc == 0, kc == KC - 1,
                   r=[("wo", n), "yTall"], x=[PK(pi)])
            TT("dve", xr[b][:, n * 512:(n + 1) * 512], psb[pi][:, :], xr[b][:, n * 512:(n + 1) * 512], ALU.add,
               r=[("xr", b)], x=[PK(pi)], w=[("xr", b)])
        so = (tb % 4) * 4
        ss = sm3[:, so:so + 1]
        rs = sm3[:, so + 1:so + 2]
        ACT(junk3[:, :], xr[b][:, :], AF.Square, r=[("xr", b)], w=["junk3", ("sm3", so)], accum_out=ss)
        ACT(rs, ss, AF.Sqrt, r=[("sm3", so), "cst"], w=[("sm3", so + 1)], scale=1.0 / D, bias=epsc)
        RCP(rs, rs, r=[("sm3", so + 1)], w=[("sm3", so + 1)])
        TS("pool", xr[b][:, :], xr[b][:, :], rs, None, ALU.mult, None, r=[("xr", b), ("sm3", so + 1)], w=[("xr", b)])
        TT("pool", xr[b][:, :], xr[b][:, :], cvec[:, V_GF:V_GF + D], ALU.mult, r=[("xr", b), "cvec"], w=[("xr", b)])
        DMA("sp", out_d[tb * 128:(tb + 1) * 128, :], xr[b][:, :], f"xr{b}", r=[("xr", b)])
    if DEBUG:
        DMA("sp", dbg_y[:, :, :], yT[:, :, :], "dbg", r=["yTall"])

    S.finish()
    return nc, S


def _consts(p, norm_w, final_norm_w, fox_out_norm_w, mlstm_out_norm_w, fox_f_bias, conv_w, conv_b,
            mlstm_i_bias, mlstm_f_bias):
    cv = np.zeros((128, CV), np.float32)
    cv[:, V_GN:V_GN + 2048] = norm_w[None, :]
    cv[:, V_GF:V_GF + 2048] = final_norm_w[None, :]
    cv[:, V_GFOX:V_GFOX + 1024] = fox_out_norm_w[None, :]
    cv[:, V_GML:V_GML + 1024] = mlstm_out_norm_w[None, :]
    gb = np.concatenate([fox_f_bias, mlstm_i_bias, mlstm_f_bias]).astype(np.float32)
    cv[:, V_GB:V_GB + 768] = np.tile(gb, 32)[None, :]
    cw = conv_w.reshape(4, 8, 128)
    cv[:, V_CW:V_CW + 32] = cw.transpose(2, 1, 0).reshape(128, 32)
    cv[:, V_CB:V_CB + 8] = conv_b.reshape(8, 128).T
    cv[:, V_SEL] = 1.0 - p
    cv[:, V_SEL + 1] = float(p)
    if p == 0:
        cv[:, V_KILL + 4:V_KILL + 8] = -30000.0
    s = np.arange(128)
    cv[:, V_TRI:V_TRI + 128] = (s[:, None] <= s[None, :]).astype(np.float32)
    cv[:, V_ONE:V_ONE + 128] = 1.0
    cm = np.zeros((128, CM), np.float32)
    cm[:, M_ID:M_ID + 128] = np.eye(128, dtype=np.float32)
    mk = np.zeros((128, 8, 512), np.float32)
    t = np.arange(512)
    for r in range(8):
        kpos = r * 128 + s
        qpos = p * 512 + t
        mk[:, r, :] = (kpos[:, None] <= qpos[None, :]).astype(np.float32)
    cm[:, M_MASK:M_MASK + 4096] = mk.reshape(128, 4096)
    return cv, cm.astype(ml_dtypes.bfloat16)


_CACHE = {}


def kernel(x, norm_w, w_in, fox_f_bias, conv_w, conv_b, mlstm_i_bias, mlstm_f_bias,
           fox_out_norm_w, mlstm_out_norm_w, w_out, final_norm_w):
    x = np.asarray(x, np.float32)
    w_in0 = np.ascontiguousarray(np.asarray(w_in, np.float32)[0])
    w_out0 = np.ascontiguousarray(np.asarray(w_out, np.float32)[0])
    if "nc" not in _CACHE:
        _CACHE["nc"] = build_program()
    nc, S = _CACHE["nc"]
    in_maps = []
    for c in range(8):
        b, p = c // 2, c % 2
        cv, cm = _consts(p, np.asarray(norm_w, np.float32)[0], np.asarray(final_norm_w, np.float32),
                         np.asarray(fox_out_norm_w, np.float32)[0], np.asarray(mlstm_out_norm_w, np.float32)[0],
                         np.asarray(fox_f_bias, np.float32)[0], np.asarray(conv_w, np.float32)[0],
                         np.asarray(conv_b, np.float32)[0], np.asarray(mlstm_i_bias, np.float32)[0],
                         np.asarray(mlstm_f_bias, np.float32)[0])
        xo = np.concatenate([x[b, 512 * (2 * m + p):512 * (2 * m + p) + 512] for m in range(4)], axis=0)
        in_maps.append({"x": np.ascontiguousarray(x[b]), "xo": np.ascontiguousarray(xo), "w_in": w_in0,
                        "w_out": w_out0, "cvec": cv, "cmat": cm})
    res = run_bass_kernel_spmd(nc, in_maps, core_ids=list(range(8)))
    _CACHE["res"] = res
    out = np.zeros((4, T, D), np.float32)
    for c in range(8):
        b, p = c // 2, c % 2
        o = res.results[c]["out"]
        for m in range(4):
            out[b, 512 * (2 * m + p):512 * (2 * m + p) + 512] = o[m * 512:(m + 1) * 512]
    return out
```

```python
import os
import numpy as np
import ml_dtypes
import concourse.bass as bass
import concourse.mybir as mybir
from concourse.bass_utils import run_bass_kernel_spmd

F32 = mybir.dt.float32
BF16 = mybir.dt.bfloat16
AF = mybir.ActivationFunctionType
ALU = mybir.AluOpType

T = 4096
D = 2048
KC = 16
TO = 2048
EPS = 1e-6
C_FQ, C_FK, C_FV, C_FZ, C_FF = 0, 1024, 2048, 3072, 4096
C_MQ, C_MK, C_MV, C_MO, C_MZ, C_MI, C_MF = 4104, 4616, 5128, 6152, 7176, 8200, 8208
NCOL = 8216
LN8 = float(np.log(8.0))

V_GN = 0
V_GF = V_GN + 2048
V_GFOX = V_GF + 2048
V_GML = V_GFOX + 1024
V_GB = V_GML + 1024
V_CW = V_GB + 768
V_CB = V_CW + 32
V_SEL = V_CB + 8
V_KILL = V_SEL + 2
V_TRI = V_KILL + 8
V_ONE = V_TRI + 128
CV = V_ONE + 128
M_ID = 0
M_MASK = 128
CM = M_MASK + 8 * 512

DEBUG = bool(int(os.environ.get("KDEBUG", "0")))
STOP_AFTER = os.environ.get("KSTOP", "")


class Ins:
    __slots__ = ("stream", "fn", "deps", "signal", "is_dma", "dkey", "sem", "thresh")


class Sched:
    def __init__(self, nc):
        self.nc = nc
        self.E = dict(pe=nc.tensor, act=nc.scalar, dve=nc.vector, pool=nc.gpsimd, sp=nc.sync)
        self.csem = {s: nc.alloc_semaphore("c_" + s) for s in ("pe", "act", "dve", "pool")}
        self.ccnt = dict.fromkeys(self.csem, 0)
        self.dsem = {}
        self.pending = []
        self.lastw = {}
        self.readers = {}
        self.waited = {s: {} for s in self.E}
        self.breq = {s: {} for s in self.E}
        self.n_ins = 0
        self.n_wait = 0

    def add(self, stream, fn, r=(), w=(), x=(), dkey=None):
        ins = Ins()
        ins.stream = stream
        ins.fn = fn
        ins.signal = False
        ins.is_dma = dkey is not None
        ins.dkey = dkey
        ins.sem = None
        ins.thresh = 0
        deps = {}

        def dep(d, raw):
            if d is None:
                return
            if d.stream == stream and not d.is_dma and not ins.is_dma:
                if stream == "pe" or not raw:
                    return
            deps[id(d)] = d

        for k in r:
            dep(self.lastw.get(k), True)
        for k in w:
            dep(self.lastw.get(k), False)
            for rd in self.readers.get(k, {}).values():
                dep(rd, False)
        for k in x:
            dep(self.lastw.get(k), False)
        rk = (stream, dkey)
        for k in r:
            self.readers.setdefault(k, {})[rk] = ins
        for k in w:
            self.lastw[k] = ins
            self.readers[k] = {}
        for k in x:
            self.lastw[k] = ins
        for d in deps.values():
            d.signal = True
        ins.deps = list(deps.values())
        self.pending.append(ins)
        return ins

    def _emit(self, ins):
        s = ins.stream
        eng = self.E[s]
        wl = self.waited[s]
        need = self.breq[s]
        self.breq[s] = {}
        for d in ins.deps:
            key = d.sem.name
            if need.get(key, (None, 0))[1] < d.thresh:
                need[key] = (d.sem, d.thresh)
        for key, (sem, val) in need.items():
            if wl.get(key, 0) < val:
                eng.wait_ge(sem, val)
                wl[key] = val
                self.n_wait += 1
        bi = ins.fn(eng)
        self.n_ins += 1
        if ins.is_dma:
            ent = self.dsem.get(ins.dkey)
            if ent is None:
                ent = [self.nc.alloc_semaphore("d_" + ins.dkey), 0]
                self.dsem[ins.dkey] = ent
            ent[1] += 16
            bi.then_inc(ent[0], 16)
            ins.sem = ent[0]
            ins.thresh = ent[1]
        elif ins.signal:
            self.ccnt[s] += 1
            bi.then_inc(self.csem[s], 1)
            ins.sem = self.csem[s]
            ins.thresh = self.ccnt[s]

    def flush(self, barrier=True):
        if barrier:
            last = {}
            for ins in self.pending:
                if not ins.is_dma:
                    last[ins.stream] = ins
            for ins in last.values():
                ins.signal = True
        for ins in self.pending:
            self._emit(ins)
        self.pending = []
        if barrier:
            for s in self.E:
                req = self.breq[s]
                for t, sem in self.csem.items():
                    if self.ccnt[t] > 0:
                        req[sem.name] = (sem, self.ccnt[t])
                for ent in self.dsem.values():
                    req[ent[0].name] = (ent[0], ent[1])
            self.lastw = {}
            self.readers = {}

    def finish(self):
        self.flush(barrier=True)
        for s in self.E:
            eng = self.E[s]
            for key, (sem, val) in self.breq[s].items():
                if self.waited[s].get(key, 0) < val:
                    eng.wait_ge(sem, val)
                    self.waited[s][key] = val
            self.breq[s] = {}


class Mem:
    def __init__(self, nc):
        self.nc = nc
        self.base = (nc.sbuf_base + 63) // 64 * 64
        self.top = nc.sbuf_top
        self.cur = self.base
        self.n = 0

    def alloc(self, name, shape, dtype):
        nbytes = int(np.prod(shape[1:])) * (4 if dtype == F32 else 2)
        nbytes = (nbytes + 63) // 64 * 64
        off = self.cur
        assert off + nbytes <= self.top, (name, off, nbytes, self.top)
        self.cur += nbytes
        self.n += 1
        return self.nc.alloc_sbuf_tensor_at(f"{name}_{self.n}", list(shape), dtype, offset=off)

    def mark(self):
        return self.cur

    def reset(self, m):
        self.cur = m


def build_program():
    nc = bass.Bass("TRN2", target_bir_lowering=False)
    S = Sched(nc)
    M = Mem(nc)

    def din(name, shape, dt):
        return nc.dram_tensor(name, list(shape), dt, kind="ExternalInput").ap()

    def dscr(name, shape, dt):
        kind = "ExternalOutput" if DEBUG else "Internal"
        return nc.dram_tensor(name, list(shape), dt, kind=kind).ap()

    x_d = din("x", [T, D], F32)
    xo_d = din("xo", [TO, D], F32)
    win_d = din("w_in", [D, NCOL], F32)
    wout_d = din("w_out", [D, D], F32)
    cvec_d = din("cvec", [128, CV], F32)
    cmat_d = din("cmat", [128, CM], BF16)
    out_d = nc.dram_tensor("out", [TO, D], F32, kind="ExternalOutput").ap()

    kT_d = dscr("s_kT", [8, 128, T], BF16)
    vF_d = dscr("s_vF", [T, 1024], BF16)
    qT_d = dscr("s_qT", [8, 128, TO], BF16)
    gz_d = dscr("s_gz", [TO, 1024], BF16)
    qmT_d = dscr("s_qmT", [4, 128, T], BF16)
    kmT_d = dscr("s_kmT", [4, 128, T], BF16)
    vM_d = dscr("s_vM", [T, 1024], BF16)
    go_d = dscr("s_go", [TO, 1024], BF16)
    gmz_d = dscr("s_gmz", [TO, 1024], BF16)
    if DEBUG:
        dbg_g = nc.dram_tensor("dbg_g", [128, 16, 768], F32, kind="ExternalOutput").ap()
        dbg_y = nc.dram_tensor("dbg_yT", [128, 16, TO], BF16, kind="ExternalOutput").ap()

    cvec = M.alloc("cvec", [128, CV], F32)
    cmat = M.alloc("cmat", [128, CM], BF16)
    graw = M.alloc("graw", [128, 32, 24], F32)
    ident = cmat[:, M_ID:M_ID + 128]
    cst = M.alloc("cst", [128, 8], F32)
    epsc = cst[:, 0:1]
    p_mark = M.mark()

    ps_all = nc.alloc_psum_tensor("ps_all", [128, 8, 512], F32)
    psb = [ps_all[:, i, :] for i in range(8)]

    def PK(i):
        return ("ps", i)

    S.add("sp", lambda e: e.dma_start(out=cvec[:, :], in_=cvec_d[:, :]), w=["cvec"], dkey="cvec")
    S.add("sp", lambda e: e.dma_start(out=cmat[:, :], in_=cmat_d[:, :]), w=["cmat"], dkey="cmat")
    S.add("dve", lambda e: e.memset(cst[:, 0:1], EPS), w=["cst"])
    S.add("dve", lambda e: e.memset(cst[:, 1:2], 1.0), w=["cst"])
    S.add("dve", lambda e: e.memset(cst[:, 2:3], -LN8), w=["cst"])

    hT = M.alloc("hT", [128, KC, 2048], BF16)
    wb = [M.alloc(f"wb{i}", [128, KC, 512], BF16) for i in range(2)]
    xt = [M.alloc(f"xt{i}", [128, D], F32) for i in range(2)]
    hn = [M.alloc(f"hn{i}", [128, D], BF16) for i in range(2)]
    junk = M.alloc("junk", [128, D], BF16)
    st = [M.alloc(f"st{i}", [128, 512], BF16) for i in range(4)]
    ub = [M.alloc(f"ub{i}", [128, 515], F32) for i in range(8)]
    cacc = [M.alloc(f"cacc{i}", [128, 512], F32) for i in range(2)]
    sm = M.alloc("sm", [128, 64], F32)
    pst_bf = [psb[6].bitcast(BF16), psb[7].bitcast(BF16)]

    for i in range(8):
        S.add("dve", (lambda i: lambda e: e.memset(ub[i][:, 0:3], 0.0))(i), w=[("ub", i)])

    st_ctr = [0]
    ps_ctr = [0]
    sm_ctr = [0]

    def next_st():
        i = st_ctr[0] % 4
        st_ctr[0] += 1
        return i

    def next_ps():
        i = ps_ctr[0] % 6
        ps_ctr[0] += 1
        return i

    def build_hT(xsrc, ntb=16):
        for tb in range(ntb):
            b = tb % 2
            S.add("sp", (lambda b, tb: lambda e: e.dma_start(out=xt[b][:, :], in_=xsrc[tb * 128:(tb + 1) * 128, :]))(b, tb),
                  w=[("xt", b)], dkey=f"xt{b}")
            so = (sm_ctr[0] % 8) * 4
            sm_ctr[0] += 1
            ss = sm[:, so:so + 1]
            rs = sm[:, so + 1:so + 2]
            S.add("act", (lambda b, ss: lambda e: e.activation(out=junk[:, :], in_=xt[b][:, :], func=AF.Square, accum_out=ss))(b, ss),
                  r=[("xt", b)], w=["junk", ("sm", so)])
            S.add("act", (lambda ss, rs: lambda e: e.activation(out=rs, in_=ss, func=AF.Sqrt, scale=1.0 / D, bias=epsc))(ss, rs),
                  r=[("sm", so), "cst"], w=[("sm", so + 1)])
            S.add("dve", (lambda rs: lambda e: e.reciprocal(out=rs, in_=rs))(rs),
                  r=[("sm", so + 1)], w=[("sm", so + 1)])
            S.add("dve", (lambda b, rs: lambda e: e.scalar_tensor_tensor(out=hn[b][:, :], in0=xt[b][:, :], scalar=rs, in1=cvec[:, V_GN:V_GN + D], op0=ALU.mult, op1=ALU.mult))(b, rs),
                  r=[("xt", b), ("sm", so + 1), "cvec"], w=[("hn", b)])
            for half in range(2):
                pb = 6 + half
                for j in range(8):
                    kc = half * 8 + j
                    S.add("pe", (lambda b, kc, j, half: lambda e: e.transpose(out=pst_bf[half][:, j * 128:(j + 1) * 128], in_=hn[b][:, kc * 128:(kc + 1) * 128], identity=ident))(b, kc, j, half),
                          r=[("hn", b), "cmat"], x=[PK(pb)])
                dst = hT[:, half * 8:(half + 1) * 8, tb * 128:(tb + 1) * 128]
                src = pst_bf[half][:, :].rearrange("p (j t) -> p j t", j=8)
                if half == 0:
                    S.add("act", (lambda dst, src: lambda e: e.copy(out=dst, in_=src))(dst, src), x=[PK(pb)], w=[("hT", tb, half)])
                else:
                    S.add("dve", (lambda dst, src: lambda e: e.tensor_copy(out=dst, in_=src))(dst, src), x=[PK(pb)], w=[("hT", tb, half)])

    def hT_keys(tbs):
        return [("hT", tb, h) for tb in tbs for h in range(2)]

    def load_w(g, cols):
        b = g % 2
        o = 0
        for (c0, n) in cols:
            src = win_d[:, c0:c0 + n].rearrange("(kc p) n -> p kc n", p=128)
            S.add("pool", (lambda b, o, n, src: lambda e: e.dma_start(out=wb[b][:, :, o:o + n], in_=src))(b, o, n, src),
                  w=[("wb", b)], dkey=f"wb{b}")
            o += n
        return b

    def mm_tok(b, tb, ncols):
        pi = next_ps()
        for kc in range(KC):
            S.add("pe", (lambda pi, kc, tb, b, ncols: lambda e: e.matmul(psb[pi][:, 0:ncols], lhsT=hT[:, kc, tb * 128:(tb + 1) * 128], rhs=wb[b][:, kc, 0:ncols], start=(kc == 0), stop=(kc == KC - 1)))(pi, kc, tb, b, ncols),
                  r=hT_keys([tb]) + [("wb", b)], x=[PK(pi)])
        return pi

    def mm_feat(b, ct, sb):
        pi = next_ps()
        for kc in range(KC):
            S.add("pe", (lambda pi, kc, ct, sb, b: lambda e: e.matmul(psb[pi][:, :], lhsT=wb[b][:, kc, ct * 128:(ct + 1) * 128], rhs=hT[:, kc, sb * 512:(sb + 1) * 512], start=(kc == 0), stop=(kc == KC - 1)))(pi, kc, ct, sb, b),
                  r=hT_keys(range(sb * 4, sb * 4 + 4)) + [("wb", b)], x=[PK(pi)])
        return pi

    def evac_store(pi, dst_ap, func=None, eng="dve"):
        si = next_st()
        if func is None:
            if eng == "dve":
                S.add("dve", (lambda pi, si: lambda e: e.tensor_copy(out=st[si][:, :], in_=psb[pi][:, :]))(pi, si), x=[PK(pi)], w=[("st", si)])
            else:
                S.add("act", (lambda pi, si: lambda e: e.copy(out=st[si][:, :], in_=psb[pi][:, :]))(pi, si), x=[PK(pi)], w=[("st", si)])
        else:
            S.add("act", (lambda pi, si, func: lambda e: e.activation(out=st[si][:, :], in_=psb[pi][:, :], func=func))(pi, si, func), x=[PK(pi)], w=[("st", si)])
        S.add("sp", (lambda si, dst_ap: lambda e: e.dma_start(out=dst_ap, in_=st[si][:, :]))(si, dst_ap), r=[("st", si)], dkey=f"st{si}")

    def conv_evac(pi, ui, dst_ap, parity):
        u = ub[ui]
        ca = cacc[parity]
        cw = lambda j: cvec[:, V_CW + ui * 4 + j:V_CW + ui * 4 + j + 1]
        cb = cvec[:, V_CB + ui:V_CB + ui + 1]
        S.add("act", (lambda pi, u: lambda e: e.copy(out=u[:, 3:515], in_=psb[pi][:, :]))(pi, u), x=[PK(pi)], w=[("ubm", ui)])
        S.add("dve", (lambda u, ca: lambda e: e.tensor_scalar(out=ca[:, :], in0=u[:, 3:515], scalar1=cw(3), scalar2=cb, op0=ALU.mult, op1=ALU.add))(u, ca),
              r=[("ubm", ui), "cvec"], w=[("cacc", parity)])
        for j in (2, 1, 0):
            S.add("dve", (lambda u, ca, j: lambda e: e.scalar_tensor_tensor(out=ca[:, :], in0=u[:, j:j + 512], scalar=cw(j), in1=ca[:, :], op0=ALU.mult, op1=ALU.add))(u, ca, j),
                  r=[("ubm", ui), ("ub", ui), ("cacc", parity)], w=[("cacc", parity)])
        S.add("dve", (lambda u: lambda e: e.tensor_copy(out=u[:, 0:3], in_=u[:, 512:515]))(u), r=[("ubm", ui)], w=[("ub", ui)])
        si = next_st()
        S.add("act", (lambda ca, si: lambda e: e.activation(out=st[si][:, :], in_=ca[:, :], func=AF.Silu))(ca, si), r=[("cacc", parity)], w=[("st", si)])
        S.add("sp", (lambda si, dst_ap: lambda e: e.dma_start(out=dst_ap, in_=st[si][:, :]))(si, dst_ap), r=[("st", si)], dkey=f"st{si}")

    gctr = [0]

    def full_pass(ps_id):
        tok0 = ps_id * 2048
        build_hT(x_d[tok0:tok0 + 2048, :])
        b = load_w(gctr[0], [(C_FF, 8), (C_MI, 16)])
        gctr[0] += 1
        for tb in range(16):
            pi = mm_tok(b, tb, 24)
            gtb = ps_id * 16 + tb
            S.add("dve", (lambda pi, gtb: lambda e: e.tensor_copy(out=graw[:, gtb, :], in_=psb[pi][:, 0:24]))(pi, gtb), x=[PK(pi)], w=[("graw", gtb)])
        for gi in range(2):
            b = load_w(gctr[0], [(C_FK + gi * 512, 512)])
            gctr[0] += 1
            for ct in range(4):
                h = gi * 4 + ct
                for sb in range(4):
                    pi = mm_feat(b, ct, sb)
                    evac_store(pi, kT_d[h, :, tok0 + sb * 512:tok0 + (sb + 1) * 512], eng=("dve" if sb % 2 else "act"))
        for gi, (c0, dd) in enumerate(((C_MQ, qmT_d), (C_MK, kmT_d))):
            b = load_w(gctr[0], [(c0, 512)])
            gctr[0] += 1
            for ct in range(4):
                for sb in range(4):
                    pi = mm_feat(b, ct, sb)
                    conv_evac(pi, gi * 4 + ct, dd[ct, :, tok0 + sb * 512:tok0 + (sb + 1) * 512], sb % 2)
        for (c0, dd) in ((C_FV, vF_d), (C_MV, vM_d)):
            for gi in range(2):
                b = load_w(gctr[0], [(c0 + gi * 512, 512)])
                gctr[0] += 1
                for tb in range(16):
                    pi = mm_tok(b, tb, 512)
                    evac_store(pi, dd[tok0 + tb * 128:tok0 + (tb + 1) * 128, gi * 512:(gi + 1) * 512], eng=("dve" if tb % 2 else "act"))

    def own_pass():
        build_hT(xo_d[:, :])
        for gi in range(2):
            b = load_w(gctr[0], [(C_FQ + gi * 512, 512)])
            gctr[0] += 1
            for ct in range(4):
                h = gi * 4 + ct
                for sb in range(4):
                    pi = mm_feat(b, ct, sb)
                    evac_store(pi, qT_d[h, :, sb * 512:(sb + 1) * 512], eng="dve")
        for (c0, dd, fn) in ((C_FZ, gz_d, AF.Silu), (C_MO, go_d, AF.Sigmoid), (C_MZ, gmz_d, AF.Silu)):
            for gi in range(2):
                b = load_w(gctr[0], [(c0 + gi * 512, 512)])
                gctr[0] += 1
                for tb in range(16):
                    pi = mm_tok(b, tb, 512)
                    evac_store(pi, dd[tb * 128:(tb + 1) * 128, gi * 512:(gi + 1) * 512], func=fn)

    full_pass(0)
    full_pass(1)
    own_pass()
    S.flush(barrier=True)
    M.reset(p_mark)

    yT_off = (M.top - KC * TO * 2) // 64 * 64
    yT = nc.alloc_sbuf_tensor_at("yT", [128, KC, TO], BF16, offset=yT_off)
    M.top = yT_off
    mask = cmat[:, M_MASK:M_MASK + 4096].rearrange("p (r t) -> p r t", r=8)
    sel0 = cvec[:, V_SEL:V_SEL + 1]
    sel1 = cvec[:, V_SEL + 1:V_SEL + 2]
    onec = cst[:, 1:2]
    nln8 = cst[:, 2:3]
    pstb = psb[7].bitcast(BF16)

    zg = M.alloc("zg", [128, 32, 24], F32)
    el = M.alloc("el", [128, 32, 24], F32)
    cum = M.alloc("cum", [128, 32, 24], F32)
    wi = M.alloc("wi", [128, 32, 24], F32)
    G = M.alloc("G", [128, 33, 24], F32)
    biasF = M.alloc("biasF", [128, 4, 32, 8], F32)
    refF = M.alloc("refF", [128, 4, 8], F32)
    t8 = M.alloc("t8", [128, 4, 8], F32)
    uu = M.alloc("uu", [128, 32, 8], F32)
    EX = M.alloc("EX", [128, 1088], F32)
    reb = EX[:, 832:1088].rearrange("p (a b) -> p a b", a=32)
    rebo = M.alloc("rebo", [128, 4, 4, 8], F32)
    wv = EX[:, 0:256].rearrange("p (a b) -> p a b", a=32)
    wh = EX[:, 256:512].rearrange("p (a b) -> p a b", a=32)
    eb = EX[:, 512:768].rearrange("p (a b) -> p a b", a=32)
    dC = EX[:, 768:800].rearrange("p (a b) -> p a b", a=4)
    dH = EX[:, 800:832].rearrange("p (a b) -> p a b", a=4)
    ebo = M.alloc("ebo", [128, 4, 4, 8], F32)
    ebt = M.alloc("ebt", [128, 4, 4, 8], F32)
    g_mark = M.mark()

    flat = lambda t: t[:, :, :].rearrange("p a b -> p (a b)")
    S.add("dve", lambda e: e.tensor_tensor(out=flat(zg), in0=flat(graw), in1=cvec[:, V_GB:V_GB + 768], op=ALU.add),
          r=["cvec"] + [("graw", i) for i in range(32)], w=["zg"])
    S.add("act", lambda e: e.activation(out=flat(el), in_=flat(zg), func=AF.Exp, scale=-1.0), r=["zg"], w=["el"])
    S.add("act", lambda e: e.activation(out=flat(el), in_=flat(el), func=AF.Ln, bias=onec, scale=1.0), r=["el", "cst"], w=["el"])
    tri = cvec[:, V_TRI:V_TRI + 128]
    ones = cvec[:, V_ONE:V_ONE + 128]
    for half in range(2):
        S.add("pe", (lambda half: lambda e: e.matmul(psb[half][:, 0:384], lhsT=ones, rhs=flat(el)[:, half * 384:(half + 1) * 384], start=True, stop=True))(half),
              r=["el", "cvec"], x=[PK(half)])
        S.add("pe", (lambda half: lambda e: e.matmul(psb[2 + half][:, 0:384], lhsT=tri, rhs=flat(el)[:, half * 384:(half + 1) * 384], start=True, stop=True))(half),
              r=["el", "cvec"], x=[PK(2 + half)])
        S.add("dve", (lambda half: lambda e: e.tensor_copy(out=flat(wi)[:, half * 384:(half + 1) * 384], in_=psb[half][:, 0:384]))(half), x=[PK(half)], w=[("wi", half)])
        S.add("dve", (lambda half: lambda e: e.tensor_copy(out=flat(cum)[:, half * 384:(half + 1) * 384], in_=psb[2 + half][:, 0:384]))(half), x=[PK(2 + half)], w=[("cumh", half)])
    S.add("dve", lambda e: e.memset(G[:, 0, :], 0.0), w=[("G", 0)])
    for kb in range(1, 33):
        S.add("dve", (lambda kb: lambda e: e.tensor_tensor(out=G[:, kb, :], in0=G[:, kb - 1, :], in1=wi[:, kb - 1, :], op=ALU.add))(kb),
              r=[("G", kb - 1), ("wi", (kb - 1) // 16)], w=[("G", kb)])
    S.add("dve", lambda e: e.tensor_tensor(out=cum[:, :, :], in0=cum[:, :, :], in1=G[:, 0:32, :], op=ALU.add),
          r=[("cumh", 0), ("cumh", 1)] + [("G", k) for k in range(33)], w=["cum"])
    for m in range(4):
        S.add("dve", (lambda m: lambda e: e.tensor_scalar(out=t8[:, m, :], in0=G[:, 8 * m + 2, 0:8], scalar1=sel0, scalar2=None, op0=ALU.mult))(m),
              r=["cum", "cvec"], w=[("t8", m)])
        S.add("dve", (lambda m: lambda e: e.scalar_tensor_tensor(out=refF[:, m, :], in0=G[:, 8 * m + 6, 0:8], scalar=sel1, in1=t8[:, m, :], op0=ALU.mult, op1=ALU.add))(m),
              r=["cum", "cvec", ("t8", m)], w=[("refF", m)])
        for kb in range(8 * m + 8):
            if kb < 8 * m:
                S.add("dve", (lambda m, kb: lambda e: e.tensor_tensor(out=biasF[:, m, kb, :], in0=cum[:, kb, 0:8], in1=refF[:, m, :], op=ALU.subtract))(m, kb),
                      r=["cum", ("refF", m)], w=["biasF"])
            else:
                kl = cvec[:, V_KILL + kb - 8 * m:V_KILL + kb - 8 * m + 1]
                S.add("dve", (lambda m, kb, kl: lambda e: e.scalar_tensor_tensor(out=biasF[:, m, kb, :], in0=cum[:, kb, 0:8], scalar=kl, in1=refF[:, m, :], op0=ALU.add, op1=ALU.subtract))(m, kb, kl),
                      r=["cum", ("refF", m), "cvec"], w=["biasF"])
    S.add("dve", lambda e: e.tensor_tensor(out=uu[:, :, :], in0=zg[:, :, 8:16], in1=cum[:, :, 16:24], op=ALU.add), r=["zg", "cum"], w=["uu"])
    for m in range(4):
        for kb in range(8 * m, 8 * m + 8):
            S.add("dve", (lambda m, kb: lambda e: e.tensor_tensor(out=wv[:, kb, :], in0=uu[:, kb, :], in1=G[:, 8 * m + 4, 16:24], op=ALU.subtract))(m, kb),
                  r=["uu", "cum"], w=["EX"])
            S.add("dve", (lambda m, kb: lambda e: e.tensor_tensor(out=wh[:, kb, :], in0=uu[:, kb, :], in1=G[:, 8 * m + 8, 16:24], op=ALU.subtract))(m, kb),
                  r=["uu", "cum"], w=["EX"])
            S.add("dve", (lambda m, kb: lambda e: e.tensor_tensor(out=eb[:, kb, :], in0=G[:, 8 * m + 4, 16:24], in1=cum[:, kb, 16:24], op=ALU.subtract))(m, kb),
                  r=["cum"], w=["EX"])
            S.add("dve", (lambda m, kb: lambda e: e.tensor_tensor(out=reb[:, kb, :], in0=cum[:, kb, 16:24], in1=G[:, 8 * m + 4, 16:24], op=ALU.subtract))(m, kb),
                  r=["cum"], w=["EX"])
        S.add("dve", (lambda m: lambda e: e.tensor_tensor(out=dC[:, m, :], in0=G[:, 8 * m, 16:24], in1=G[:, 8 * m + 8, 16:24], op=ALU.subtract))(m), r=["cum"], w=["EX"])
        S.add("dve", (lambda m: lambda e: e.tensor_tensor(out=dH[:, m, :], in0=G[:, 8 * m, 16:24], in1=G[:, 8 * m + 4, 16:24], op=ALU.subtract))(m), r=["cum"], w=["EX"])
    S.add("act", lambda e: e.activation(out=EX[:, 0:512], in_=EX[:, 0:512], func=AF.Exp, bias=nln8, scale=1.0), r=["EX", "cst"], w=["EX"])
    S.add("act", lambda e: e.activation(out=EX[:, 512:1088], in_=EX[:, 512:1088], func=AF.Exp), r=["EX"], w=["EX"])
    for m in range(4):
        S.add("dve", (lambda m: lambda e: e.tensor_scalar(out=ebt[:, m, :, :], in0=eb[:, 8 * m:8 * m + 4, :], scalar1=sel0, scalar2=None, op0=ALU.mult))(m),
              r=["EX", "cvec"], w=[("ebt", m)])
        S.add("dve", (lambda m: lambda e: e.scalar_tensor_tensor(out=ebo[:, m, :, :], in0=eb[:, 8 * m + 4:8 * m + 8, :], scalar=sel1, in1=ebt[:, m, :, :], op0=ALU.mult, op1=ALU.add))(m),
              r=["EX", "cvec", ("ebt", m)], w=["ebo"])
    for m in range(4):
        S.add("dve", (lambda m: lambda e: e.tensor_scalar(out=ebt[:, m, :, :], in0=reb[:, 8 * m:8 * m + 4, :], scalar1=sel0, scalar2=None, op0=ALU.mult))(m),
              r=["EX", "cvec", "ebo"], w=[("ebt", m)])
        S.add("dve", (lambda m: lambda e: e.scalar_tensor_tensor(out=rebo[:, m, :, :], in0=reb[:, 8 * m + 4:8 * m + 8, :], scalar=sel1, in1=ebt[:, m, :, :], op0=ALU.mult, op1=ALU.add))(m),
              r=["EX", "cvec", ("ebt", m)], w=["rebo"])
    S.flush(barrier=True)

    if DEBUG:
        dl = [(0, graw), (1, zg), (2, el), (3, cum), (4, wi)]
        for i, t in dl:
            S.add("sp", (lambda i, t: lambda e: e.dma_start(out=dbg_g[:, i, :], in_=flat(t)))(i, t), dkey="dbg")
        S.add("sp", lambda e: e.dma_start(out=dbg_g[:, 5, :], in_=G[:, 0:32, :].rearrange("p a b -> p (a b)")), dkey="dbg")
        S.add("sp", lambda e: e.dma_start(out=dbg_g[:, 6, :], in_=EX[:, 0:768]), dkey="dbg")
        S.add("sp", lambda e: e.dma_start(out=dbg_g[:, 7, 0:64], in_=EX[:, 768:832]), dkey="dbg")
        S.add("sp", lambda e: e.dma_start(out=dbg_g[:, 7, 64:192], in_=ebo[:, :, :, :].rearrange("p a b c -> p (a b c)")), dkey="dbg")
        for m in range(4):
            S.add("sp", (lambda m: lambda e: e.dma_start(out=dbg_g[:, 8 + m, 0:256], in_=biasF[:, m, :, :].rearrange("p a b -> p (a b)")))(m), dkey="dbg")
        S.flush(barrier=True)
    if STOP_AFTER in ("A", "B"):
        S.finish()
        return nc, S

    def MM(out, lhsT, rhs, start, stop, r, x):
        S.add("pe", lambda e: e.matmul(out, lhsT=lhsT, rhs=rhs, start=start, stop=stop), r=r, x=x)

    def TR(out, in_, r, x):
        S.add("pe", lambda e: e.transpose(out=out, in_=in_, identity=ident), r=list(r) + ["cmat"], x=x)

    def ACT(out, in_, func, r=(), w=(), x=(), **kw):
        S.add("act", lambda e: e.activation(out=out, in_=in_, func=func, **kw), r=r, w=w, x=x)

    def TS(eng, out, in0, s1, s2, op0, op1, r=(), w=(), x=()):
        if op1 is None:
            S.add(eng, lambda e: e.tensor_scalar(out=out, in0=in0, scalar1=s1, scalar2=None, op0=op0), r=r, w=w, x=x)
        else:
            S.add(eng, lambda e: e.tensor_scalar(out=out, in0=in0, scalar1=s1, scalar2=s2, op0=op0, op1=op1), r=r, w=w, x=x)

    def STT(eng, out, in0, scalar, in1, op0, op1, r=(), w=(), x=()):
        S.add(eng, lambda e: e.scalar_tensor_tensor(out=out, in0=in0, scalar=scalar, in1=in1, op0=op0, op1=op1), r=r, w=w, x=x)

    def TT(eng, out, in0, in1, op, r=(), w=(), x=()):
        S.add(eng, lambda e: e.tensor_tensor(out=out, in0=in0, in1=in1, op=op), r=r, w=w, x=x)

    def CP(eng, out, in_, r=(), w=(), x=()):
        S.add(eng, lambda e: e.tensor_copy(out=out, in_=in_), r=r, w=w, x=x)

    def RCP(out, in_, r=(), w=(), x=()):
        S.add("dve", lambda e: e.reciprocal(out=out, in_=in_), r=r, w=w, x=x)

    def DMA(eng, out, in_, dkey, r=(), w=()):
        S.add(eng, lambda e: e.dma_start(out=out, in_=in_), r=r, w=w, dkey=dkey)

    def MS(eng, ap, val, w):
        S.add(eng, lambda e: e.memset(ap, val), w=w)

    SCALE = 128.0 ** -0.5
    kTh = [M.alloc(f"kTh{i}", [128, T], BF16) for i in range(2)]
    vh = [M.alloc(f"vh{i}", [128, 32, 132], BF16) for i in range(2)]
    qTh = [M.alloc(f"qTh{i}", [128, TO], BF16) for i in range(2)]
    gzh = [M.alloc(f"gzh{i}", [128, 16, 128], BF16) for i in range(2)]
    pT = [M.alloc(f"pT{i}", [128, 512], BF16) for i in range(4)]
    ctr = dict(pT=0, t1=0, yb=0, sm=0, hb=0, o4=0)

    def rot(name, n):
        i = ctr[name] % n
        ctr[name] += 1
        return i

    def alloc_small(tag):
        d = {}
        d["t1"] = [M.alloc(f"t1{tag}{i}", [128, 128], F32) for i in range(4)]
        d["hb"] = [M.alloc(f"hb{tag}{i}", [128, 128], F32) for i in range(4)]
        d["yb"] = [M.alloc(f"yb{tag}{i}", [128, 128], BF16) for i in range(8)]
        d["junk"] = M.alloc(f"junk2{tag}", [128, 128], F32)
        d["sm"] = M.alloc(f"sm2{tag}", [128, 64], F32)
        d["o4"] = [M.alloc(f"o4{tag}{i}", [128, 4, 132], F32) for i in range(2)]
        return d

    def epilogue4(sb, srcs, src_r, src_x, sc4, sc_keys, gates, gate_key, gvec_off, kc_out, m, yb_eng):
        sm2 = sb["sm"]
        so = rot("sm", 4) * 16
        slot = ("sm2", so)
        ssq4 = sm2[:, so + 4:so + 8]
        r4 = sm2[:, so + 8:so + 12]
        rr4 = sm2[:, so + 12:so + 16]
        for c in range(4):
            kw = dict(accum_out=ssq4[:, c:c + 1])
            if sc4 is not None:
                kw["scale"] = sc4[:, c:c + 1]
            ACT(sb["junk"][:, :], srcs[c], AF.Square, r=list(sc_keys) + list(src_r[c]), x=src_x[c], w=["junk2", (slot, "ssq")], **kw)
        ACT(r4, ssq4, AF.Sqrt, r=[(slot, "ssq"), "cst"], w=[(slot, "r")], scale=1.0 / 128, bias=epsc)
        RCP(r4, r4, r=[(slot, "r")], w=[(slot, "r")])
        if sc4 is not None:
            TT("dve", rr4, r4, sc4, ALU.mult, r=[(slot, "r")] + list(sc_keys), w=[(slot, "rr")])
            fin, fk = rr4, (slot, "rr")
        else:
            fin, fk = r4, (slot, "r")
        for c in range(4):
            ti = rot("t1", 4)
            STT("dve", sb["t1"][ti][:, :], srcs[c], fin[:, c:c + 1], cvec[:, gvec_off:gvec_off + 128], ALU.mult, ALU.mult,
                r=[fk, "cvec"] + list(src_r[c]), x=src_x[c], w=[("t1", ti)])
            yi = rot("yb", 8)
            TT(yb_eng, sb["yb"][yi][:, :], sb["t1"][ti][:, :], gates[c], ALU.mult, r=[("t1", ti), gate_key], w=[("yb", yi)])
            TR(pstb[:, c * 128:(c + 1) * 128], sb["yb"][yi][:, :], r=[("yb", yi)], x=[PK(7)])
        CP("dve", yT[:, kc_out, m * 512:(m + 1) * 512], pstb[:, 0:512], x=[PK(7)], w=[("yT", kc_out, m)])

    sbD = alloc_small("d")
    for b in range(2):
        MS("pool", vh[b][:, :, 128:129], 1.0, w=[("vh1", b)])

    def fox_head(h):
        b = h % 2
        DMA("sp", kTh[b][:, :], kT_d[h, :, :], f"kTh{b}", w=[("kTh", b)])
        DMA("sp", qTh[b][:, :], qT_d[h, :, :], f"qTh{b}", w=[("qTh", b)])
        DMA("sp", vh[b][:, :, 0:128], vF_d[:, h * 128:(h + 1) * 128].rearrange("(kb p) d -> p kb d", p=128), f"vh{b}", w=[("vh", b)])
        DMA("sp", gzh[b][:, :, :], gz_d[:, h * 128:(h + 1) * 128].rearrange("(tb p) d -> p tb d", p=128), f"gzh{b}", w=[("gzh", b)])
        for m in range(4):
            nkb = 8 * m + 8

            def emit_S(kb):
                pi = kb % 3
                MM(psb[pi][:, :], kTh[b][:, kb * 128:(kb + 1) * 128], qTh[b][:, m * 512:(m + 1) * 512], True, True,
                   r=[("kTh", b), ("qTh", b)], x=[PK(pi)])

            emit_S(0)
            emit_S(1)
            for kb in range(nkb):
                if kb + 2 < nkb:
                    emit_S(kb + 2)
                pi = kb % 3
                pj = rot("pT", 4)
                ACT(pT[pj][:, :], psb[pi][:, :], AF.Exp, r=["biasF"], x=[PK(pi)], w=[("pT", pj)],
                    bias=biasF[:, m, kb, h:h + 1], scale=SCALE)
                if kb >= 8 * m:
                    TT("dve", pT[pj][:, :], pT[pj][:, :], mask[:, kb - 8 * m, :], ALU.mult, r=[("pT", pj), "cmat"], w=[("pT", pj)])
                for c in range(4):
                    MM(psb[3 + c][:, 0:129], pT[pj][:, c * 128:(c + 1) * 128], vh[b][:, kb, 0:129], kb == 0, kb == nkb - 1,
                       r=[("pT", pj), ("vh", b), ("vh1", b)], x=[PK(3 + c)])
            oi = rot("o4", 2)
            O4 = sbD["o4"][oi]
            CP("dve", O4[:, :, 0:129], ps_all[:, 3:7, 0:129], x=[PK(3), PK(4), PK(5), PK(6)], w=[("o4", oi)])
            so = rot("sm", 4) * 16
            rl4 = sbD["sm"][:, so:so + 4]
            RCP(rl4, O4[:, :, 128], r=[("o4", oi)], w=[(("sm2", so), "rl")])
            epilogue4(sbD, [O4[:, c, 0:128] for c in range(4)], [[("o4", oi)]] * 4, [[]] * 4,
                      rl4, [(("sm2", so), "rl")], [gzh[b][:, 4 * m + c, :] for c in range(4)], ("gzh", b), V_GFOX + h * 128, h, m, "dve")

    for h in range(8):
        fox_head(h)
    S.flush(barrier=True)
    M.reset(g_mark)
    if DEBUG and STOP_AFTER == "D":
        DMA("sp", dbg_y[:, :, :], yT[:, :, :], "dbg")
        S.flush(barrier=True)
    if STOP_AFTER == "D":
        S.finish()
        return nc, S

    kmT = M.alloc("kmT", [128, T], BF16)
    qmT = M.alloc("qmT", [128, T], BF16)
    qown = M.alloc("qown", [128, TO], BF16)
    qtmp = M.alloc("qtmp", [128, 512], BF16)
    Kh = M.alloc("Kh", [128, 32, 128], BF16)
    vmh = [M.alloc(f"vmh{i}", [128, 32, 132], BF16) for i in range(2)]
    goh = [M.alloc(f"goh{i}", [128, 16, 128], BF16) for i in range(2)]
    gmzh = [M.alloc(f"gmzh{i}", [128, 16, 128], BF16) for i in range(2)]
    aT = [M.alloc(f"aT{i}", [128, 512], BF16) for i in range(4)]
    Cst = M.alloc("Cst", [128, 132], F32)
    Ct = M.alloc("Ct", [128, 132], BF16)
    sbE = alloc_small("e")
    for b in range(2):
        MS("pool", vmh[b][:, :, 128:129], 1.0, w=[("vmh1", b)])

    def mlstm_ct(ct):
        DMA("sp", kmT[:, :], kmT_d[ct, :, :], "kmT", w=["kmT"])
        DMA("sp", qmT[:, :], qmT_d[ct, :, :], "qmT", w=["qmT"])
        for m in range(4):
            TS("pool", qown[:, m * 512:(m + 1) * 512], qmT[:, 1024 * m:1024 * m + 512], sel0, None, ALU.mult, None,
               r=["qmT", "cvec"], w=[("qown", m)])
            TS("pool", qtmp[:, :], qmT[:, 1024 * m + 512:1024 * m + 1024], sel1, None, ALU.mult, None,
               r=["qmT", "cvec"], w=["qtmp"])
            TT("pool", qown[:, m * 512:(m + 1) * 512], qown[:, m * 512:(m + 1) * 512], qtmp[:, :], ALU.add,
               r=["qtmp", ("qown", m)], w=[("qown", m)])
        for t0 in range(0, 32, 8):
            for j in range(8):
                tb = t0 + j
                TR(pstb[:, j * 128:(j + 1) * 128], kmT[:, tb * 128:(tb + 1) * 128], r=["kmT"], x=[PK(7)])
            for j in range(8):
                tb = t0 + j
                for hh in range(2):
                    h = 2 * ct + hh
                    ACT(Kh[:, tb, hh * 64:(hh + 1) * 64], pstb[:, j * 128 + hh * 64:j * 128 + (hh + 1) * 64], AF.Copy,
                        r=["EXr"], x=[PK(7)], w=[("Kh", tb)], scale=wh[:, tb, h:h + 1])
        for hh in range(2):
            h = 2 * ct + hh
            b = h % 2
            po = 64 * hh
            DMA("sp", vmh[b][:, :, 0:128], vM_d[:, h * 128:(h + 1) * 128].rearrange("(kb p) d -> p kb d", p=128), f"vmh{b}", w=[("vmh", b)])
            DMA("sp", goh[b][:, :, :], go_d[:, h * 128:(h + 1) * 128].rearrange("(tb p) d -> p tb d", p=128), f"goh{b}", w=[("goh", b)])
            DMA("sp", gmzh[b][:, :, :], gmz_d[:, h * 128:(h + 1) * 128].rearrange("(tb p) d -> p tb d", p=128), f"gmzh{b}", w=[("gmzh", b)])
            MS("dve", Cst[:, :], 0.0, w=["Cst"])
            for m in range(4):
                TS("dve", Ct[po:po + 64, 0:129], Cst[po:po + 64, 0:129], dH[po:po + 64, m, h:h + 1], None, ALU.mult, None,
                   r=["Cst", "EXr"], w=["Ct"])
                for r_ in range(8):
                    kb = 8 * m + r_
                    pi = r_ % 2
                    MM(psb[pi][:, :], kmT[po:po + 64, kb * 128:(kb + 1) * 128], qown[po:po + 64, m * 512:(m + 1) * 512], True, True,
                       r=["kmT", ("qown", m)], x=[PK(pi)])
                    aj = rot("pT", 4)
                    STT("dve", aT[aj][:, :], psb[pi][:, :], wv[:, kb, h:h + 1], mask[:, r_, :], ALU.mult, ALU.mult,
                        r=["EXr", "cmat"], x=[PK(pi)], w=[("aT", aj)])
                    for c in range(4):
                        MM(psb[3 + c][:, 0:129], aT[aj][:, c * 128:(c + 1) * 128], vmh[b][:, kb, 0:129], r_ == 0, False,
                           r=[("aT", aj), ("vmh", b), ("vmh1", b)], x=[PK(3 + c)])
                for c in range(4):
                    MM(psb[3 + c][:, 0:129], qown[po:po + 64, m * 512 + c * 128:m * 512 + (c + 1) * 128], Ct[po:po + 64, 0:129], False, True,
                       r=[("qown", m), "Ct"], x=[PK(3 + c)])
                sm2 = sbE["sm"]
                so = rot("sm", 4) * 16
                a4 = sm2[:, so:so + 4]
                k_a = (("sm2", so), "rl")
                xo4 = [PK(3), PK(4), PK(5), PK(6)]
                oi = rot("o4", 2)
                O4 = sbE["o4"][oi]
                CP("dve", O4[:, :, 0:129], ps_all[:, 3:7, 0:129], x=xo4, w=[("o4", oi)])
                STT("dve", a4, O4[:, :, 128], -1.0, O4[:, :, 128], ALU.mult, ALU.max, r=[("o4", oi)], w=[k_a])
                TT("dve", a4, a4, rebo[:, m, :, h], ALU.max, r=[k_a, "EXr"], w=[k_a])
                RCP(a4, a4, r=[k_a], w=[k_a])
                his = []
                for c in range(4):
                    hi = rot("hb", 4)
                    his.append(hi)
                    STT("dve", sbE["hb"][hi][:, :], O4[:, c, 0:128], a4[:, c:c + 1], goh[b][:, 4 * m + c, :], ALU.mult, ALU.mult,
                        r=[k_a, ("goh", b), ("o4", oi)], w=[("hb", hi)])
                epilogue4(sbE, [sbE["hb"][hi][:, :] for hi in his], [[("hb", hi)] for hi in his], [[]] * 4,
                          None, [], [gmzh[b][:, 4 * m + c, :] for c in range(4)], ("gmzh", b), V_GML + h * 128, 8 + h, m, "pool")
                if m < 3:
                    for r_ in range(8):
                        kb = 8 * m + r_
                        MM(psb[2][:, 0:129], Kh[:, kb, :], vmh[b][:, kb, 0:129], r_ == 0, r_ == 7,
                           r=[("Kh", kb), ("vmh", b), ("vmh1", b)], x=[PK(2)])
                    STT("dve", Cst[po:po + 64, 0:129], Cst[po:po + 64, 0:129], dC[po:po + 64, m, h:h + 1], psb[2][po:po + 64, 0:129], ALU.mult, ALU.add,
                        r=["Cst", "EXr"], x=[PK(2)], w=["Cst"])

    for ct in range(4):
        mlstm_ct(ct)
    S.flush(barrier=True)
    M.reset(p_mark)
    if DEBUG and STOP_AFTER == "E":
        DMA("sp", dbg_y[:, :, :], yT[:, :, :], "dbg")
        S.flush(barrier=True)
    if STOP_AFTER == "E":
        S.finish()
        return nc, S

    wo = M.alloc("wo", [128, KC, D], BF16)
    xr = [M.alloc(f"xr{i}", [128, D], F32) for i in range(2)]
    junk3 = M.alloc("junk3", [128, D], BF16)
    sm3 = M.alloc("sm3", [128, 16], F32)
    for n in range(4):
        DMA("pool", wo[:, :, n * 512:(n + 1) * 512], wout_d[:, n * 512:(n + 1) * 512].rearrange("(kc p) n -> p kc n", p=128), f"wo{n}", w=[("wo", n)])
    pctr = 0
    for tb in range(16):
        b = tb % 2
        DMA("sp", xr[b][:, :], xo_d[tb * 128:(tb + 1) * 128, :], f"xr{b}", w=[("xr", b)])
        for n in range(4):
            pi = pctr % 6
            pctr += 1
            for kc in range(KC):
                MM(psb[pi][:, :], yT[:, kc, tb * 128:(tb + 1) * 128], wo[:, kc, n * 512:(n + 1) * 512], kc == 0, kc == KC - 1,
                   r=[("wo", n), "yTall"], x=[PK(pi)])
            TT("dve", xr[b][:, n * 512:(n + 1) * 512], psb[pi][:, :], xr[b][:, n * 512:(n + 1) * 512], ALU.add,
               r=[("xr", b)], x=[PK(pi)], w=[("xr", b)])
        so = (tb % 4) * 4
        ss = sm3[:, so:so + 1]
        rs = sm3[:, so + 1:so + 2]
        ACT(junk3[:, :], xr[b][:, :], AF.Square, r=[("xr", b)], w=["junk3", ("sm3", so)], accum_out=ss)
        ACT(rs, ss, AF.Sqrt, r=[("sm3", so), "cst"], w=[("sm3", so + 1)], scale=1.0 / D, bias=epsc)
        RCP(rs, rs, r=[("sm3", so + 1)], w=[("sm3", so + 1)])
        STT("dve", xr[b][:, :], xr[b][:, :], rs, cvec[:, V_GF:V_GF + D], ALU.mult, ALU.mult,
            r=[("xr", b), ("sm3", so + 1), "cvec"], w=[("xr", b)])
        DMA("sp", out_d[tb * 128:(tb + 1) * 128, :], xr[b][:, :], f"xr{b}", r=[("xr", b)])
    if DEBUG:
        DMA("sp", dbg_y[:, :, :], yT[:, :, :], "dbg", r=["yTall"])

    S.finish()
    return nc, S


def _consts(p, norm_w, final_norm_w, fox_out_norm_w, mlstm_out_norm_w, fox_f_bias, conv_w, conv_b,
            mlstm_i_bias, mlstm_f_bias):
    cv = np.zeros((128, CV), np.float32)
    cv[:, V_GN:V_GN + 2048] = norm_w[None, :]
    cv[:, V_GF:V_GF + 2048] = final_norm_w[None, :]
    cv[:, V_GFOX:V_GFOX + 1024] = fox_out_norm_w[None, :]
    cv[:, V_GML:V_GML + 1024] = mlstm_out_norm_w[None, :]
    gb = np.concatenate([fox_f_bias, mlstm_i_bias, mlstm_f_bias]).astype(np.float32)
    cv[:, V_GB:V_GB + 768] = np.tile(gb, 32)[None, :]
    cw = conv_w.reshape(4, 8, 128)
    cv[:, V_CW:V_CW + 32] = cw.transpose(2, 1, 0).reshape(128, 32)
    cv[:, V_CB:V_CB + 8] = conv_b.reshape(8, 128).T
    cv[:, V_SEL] = 1.0 - p
    cv[:, V_SEL + 1] = float(p)
    if p == 0:
        cv[:, V_KILL + 4:V_KILL + 8] = -30000.0
    s = np.arange(128)
    cv[:, V_TRI:V_TRI + 128] = (s[:, None] <= s[None, :]).astype(np.float32)
    cv[:, V_ONE:V_ONE + 128] = 1.0
    cm = np.zeros((128, CM), np.float32)
    cm[:, M_ID:M_ID + 128] = np.eye(128, dtype=np.float32)
    mk = np.zeros((128, 8, 512), np.float32)
    t = np.arange(512)
    for r in range(8):
        kpos = r * 128 + s
        qpos = p * 512 + t
        mk[:, r, :] = (kpos[:, None] <= qpos[None, :]).astype(np.float32)
    cm[:, M_MASK:M_MASK + 4096] = mk.reshape(128, 4096)
    return cv, cm.astype(ml_dtypes.bfloat16)


_CACHE = {}


def kernel(x, norm_w, w_in, fox_f_bias, conv_w, conv_b, mlstm_i_bias, mlstm_f_bias,
           fox_out_norm_w, mlstm_out_norm_w, w_out, final_norm_w):
    x = np.asarray(x, np.float32)
    w_in0 = np.ascontiguousarray(np.asarray(w_in, np.float32)[0])
    w_out0 = np.ascontiguousarray(np.asarray(w_out, np.float32)[0])
    if "nc" not in _CACHE:
        _CACHE["nc"] = build_program()
    nc, S = _CACHE["nc"]
    in_maps = []
    for c in range(8):
        b, p = c // 2, c % 2
        cv, cm = _consts(p, np.asarray(norm_w, np.float32)[0], np.asarray(final_norm_w, np.float32),
                         np.asarray(fox_out_norm_w, np.float32)[0], np.asarray(mlstm_out_norm_w, np.float32)[0],
                         np.asarray(fox_f_bias, np.float32)[0], np.asarray(conv_w, np.float32)[0],
                         np.asarray(conv_b, np.float32)[0], np.asarray(mlstm_i_bias, np.float32)[0],
                         np.asarray(mlstm_f_bias, np.float32)[0])
        xo = np.concatenate([x[b, 512 * (2 * m + p):512 * (2 * m + p) + 512] for m in range(4)], axis=0)
        in_maps.append({"x": np.ascontiguousarray(x[b]), "xo": np.ascontiguousarray(xo), "w_in": w_in0,
                        "w_out": w_out0, "cvec": cv, "cmat": cm})
    res = run_bass_kernel_spmd(nc, in_maps, core_ids=list(range(8)))
    _CACHE["res"] = res
    out = np.zeros((4, T, D), np.float32)
    for c in range(8):
        b, p = c // 2, c % 2
        o = res.results[c]["out"]
        for m in range(4):
            out[b, 512 * (2 * m + p):512 * (2 * m + p) + 512] = o[m * 512:(m + 1) * 512]
    return out
```

```python
import os
import numpy as np
import ml_dtypes
import concourse.bass as bass
import concourse.mybir as mybir
from concourse.bass_utils import run_bass_kernel_spmd

F32 = mybir.dt.float32
BF16 = mybir.dt.bfloat16
AF = mybir.ActivationFunctionType
ALU = mybir.AluOpType

T = 4096
D = 2048
KC = 16
TO = 2048
EPS = 1e-6
C_FQ, C_FK, C_FV, C_FZ, C_FF = 0, 1024, 2048, 3072, 4096
C_MQ, C_MK, C_MV, C_MO, C_MZ, C_MI, C_MF = 4104, 4616, 5128, 6152, 7176, 8200, 8208
NCOL = 8216
LN8 = float(np.log(8.0))

V_GN = 0
V_GF = V_GN + 2048
V_GFOX = V_GF + 2048
V_GML = V_GFOX + 1024
V_GB = V_GML + 1024
V_CW = V_GB + 768
V_CB = V_CW + 32
V_SEL = V_CB + 8
V_KILL = V_SEL + 2
V_TRI = V_KILL + 8
V_ONE = V_TRI + 128
CV = V_ONE + 128
M_ID = 0
M_MASK = 128
CM = M_MASK + 8 * 512

DEBUG = bool(int(os.environ.get("KDEBUG", "0")))
STOP_AFTER = os.environ.get("KSTOP", "")


class Ins:
    __slots__ = ("stream", "fn", "deps", "signal", "is_dma", "dkey", "sem", "thresh")


class Sched:
    def __init__(self, nc):
        self.nc = nc
        self.E = dict(pe=nc.tensor, act=nc.scalar, dve=nc.vector, pool=nc.gpsimd, sp=nc.sync)
        self.csem = {s: nc.alloc_semaphore("c_" + s) for s in ("pe", "act", "dve", "pool")}
        self.ccnt = dict.fromkeys(self.csem, 0)
        self.dsem = {}
        self.pending = []
        self.lastw = {}
        self.readers = {}
        self.waited = {s: {} for s in self.E}
        self.breq = {s: {} for s in self.E}
        self.n_ins = 0
        self.n_wait = 0

    def add(self, stream, fn, r=(), w=(), x=(), dkey=None):
        ins = Ins()
        ins.stream = stream
        ins.fn = fn
        ins.signal = False
        ins.is_dma = dkey is not None
        ins.dkey = dkey
        ins.sem = None
        ins.thresh = 0
        deps = {}

        def dep(d, raw):
            if d is None:
                return
            if d.stream == stream and not d.is_dma and not ins.is_dma:
                if stream == "pe" or not raw:
                    return
            deps[id(d)] = d

        for k in r:
            dep(self.lastw.get(k), True)
        for k in w:
            dep(self.lastw.get(k), False)
            for rd in self.readers.get(k, {}).values():
                dep(rd, False)
        for k in x:
            dep(self.lastw.get(k), False)
        rk = (stream, dkey)
        for k in r:
            self.readers.setdefault(k, {})[rk] = ins
        for k in w:
            self.lastw[k] = ins
            self.readers[k] = {}
        for k in x:
            self.lastw[k] = ins
        for d in deps.values():
            d.signal = True
        ins.deps = list(deps.values())
        self.pending.append(ins)
        return ins

    def _emit(self, ins):
        s = ins.stream
        eng = self.E[s]
        wl = self.waited[s]
        need = self.breq[s]
        self.breq[s] = {}
        for d in ins.deps:
            key = d.sem.name
            if need.get(key, (None, 0))[1] < d.thresh:
                need[key] = (d.sem, d.thresh)
        for key, (sem, val) in need.items():
            if wl.get(key, 0) < val:
                eng.wait_ge(sem, val)
                wl[key] = val
                self.n_wait += 1
        bi = ins.fn(eng)
        self.n_ins += 1
        if ins.is_dma:
            ent = self.dsem.get(ins.dkey)
            if ent is None:
                ent = [self.nc.alloc_semaphore("d_" + ins.dkey), 0]
                self.dsem[ins.dkey] = ent
            ent[1] += 16
            bi.then_inc(ent[0], 16)
            ins.sem = ent[0]
            ins.thresh = ent[1]
        elif ins.signal:
            self.ccnt[s] += 1
            bi.then_inc(self.csem[s], 1)
            ins.sem = self.csem[s]
            ins.thresh = self.ccnt[s]

    def flush(self, barrier=True):
        if barrier:
            last = {}
            for ins in self.pending:
                if not ins.is_dma:
                    last[ins.stream] = ins
            for ins in last.values():
                ins.signal = True
        for ins in self.pending:
            self._emit(ins)
        self.pending = []
        if barrier:
            for s in self.E:
                req = self.breq[s]
                for t, sem in self.csem.items():
                    if self.ccnt[t] > 0:
                        req[sem.name] = (sem, self.ccnt[t])
                for ent in self.dsem.values():
                    req[ent[0].name] = (ent[0], ent[1])
            self.lastw = {}
            self.readers = {}

    def finish(self):
        self.flush(barrier=True)
        for s in self.E:
            eng = self.E[s]
            for key, (sem, val) in self.breq[s].items():
                if self.waited[s].get(key, 0) < val:
                    eng.wait_ge(sem, val)
                    self.waited[s][key] = val
            self.breq[s] = {}


class Mem:
    def __init__(self, nc):
        self.nc = nc
        self.base = (nc.sbuf_base + 63) // 64 * 64
        self.top = nc.sbuf_top
        self.cur = self.base
        self.n = 0

    def alloc(self, name, shape, dtype):
        nbytes = int(np.prod(shape[1:])) * (4 if dtype == F32 else 2)
        nbytes = (nbytes + 63) // 64 * 64
        off = self.cur
        assert off + nbytes <= self.top, (name, off, nbytes, self.top)
        self.cur += nbytes
        self.n += 1
        return self.nc.alloc_sbuf_tensor_at(f"{name}_{self.n}", list(shape), dtype, offset=off)

    def mark(self):
        return self.cur

    def reset(self, m):
        self.cur = m


def build_program():
    nc = bass.Bass("TRN2", target_bir_lowering=False)
    S = Sched(nc)
    M = Mem(nc)

    def din(name, shape, dt):
        return nc.dram_tensor(name, list(shape), dt, kind="ExternalInput").ap()

    def dscr(name, shape, dt):
        kind = "ExternalOutput" if DEBUG else "Internal"
        return nc.dram_tensor(name, list(shape), dt, kind=kind).ap()

    x_d = din("x", [T, D], F32)
    xo_d = din("xo", [TO, D], F32)
    win_d = din("w_in", [D, NCOL], F32)
    wout_d = din("w_out", [D, D], F32)
    cvec_d = din("cvec", [128, CV], F32)
    cmat_d = din("cmat", [128, CM], BF16)
    out_d = nc.dram_tensor("out", [TO, D], F32, kind="ExternalOutput").ap()

    kT_d = dscr("s_kT", [8, 128, T], BF16)
    vF_d = dscr("s_vF", [T, 1024], BF16)
    qT_d = dscr("s_qT", [8, 128, TO], BF16)
    gz_d = dscr("s_gz", [TO, 1024], BF16)
    qmT_d = dscr("s_qmT", [4, 128, T], BF16)
    kmT_d = dscr("s_kmT", [4, 128, T], BF16)
    vM_d = dscr("s_vM", [T, 1024], BF16)
    go_d = dscr("s_go", [TO, 1024], BF16)
    gmz_d = dscr("s_gmz", [TO, 1024], BF16)
    if DEBUG:
        dbg_g = nc.dram_tensor("dbg_g", [128, 16, 768], F32, kind="ExternalOutput").ap()
        dbg_y = nc.dram_tensor("dbg_yT", [128, 16, TO], BF16, kind="ExternalOutput").ap()

    cvec = M.alloc("cvec", [128, CV], F32)
    cmat = M.alloc("cmat", [128, CM], BF16)
    graw = M.alloc("graw", [128, 32, 24], F32)
    ident = cmat[:, M_ID:M_ID + 128]
    cst = M.alloc("cst", [128, 8], F32)
    epsc = cst[:, 0:1]
    p_mark = M.mark()

    ps_all = nc.alloc_psum_tensor("ps_all", [128, 8, 512], F32)
    psb = [ps_all[:, i, :] for i in range(8)]

    def PK(i):
        return ("ps", i)

    S.add("sp", lambda e: e.dma_start(out=cvec[:, :], in_=cvec_d[:, :]), w=["cvec"], dkey="cvec")
    S.add("sp", lambda e: e.dma_start(out=cmat[:, :], in_=cmat_d[:, :]), w=["cmat"], dkey="cmat")
    S.add("dve", lambda e: e.memset(cst[:, 0:1], EPS), w=["cst"])
    S.add("dve", lambda e: e.memset(cst[:, 1:2], 1.0), w=["cst"])
    S.add("dve", lambda e: e.memset(cst[:, 2:3], -LN8), w=["cst"])

    hT = M.alloc("hT", [128, KC, 2048], BF16)
    wb = [M.alloc(f"wb{i}", [128, KC, 512], BF16) for i in range(2)]
    xt = [M.alloc(f"xt{i}", [128, D], F32) for i in range(2)]
    hn = [M.alloc(f"hn{i}", [128, D], BF16) for i in range(2)]
    junk = M.alloc("junk", [128, D], BF16)
    st = [M.alloc(f"st{i}", [128, 512], BF16) for i in range(4)]
    ub = [M.alloc(f"ub{i}", [128, 515], F32) for i in range(8)]
    cacc = [M.alloc(f"cacc{i}", [128, 512], F32) for i in range(2)]
    sm = M.alloc("sm", [128, 64], F32)
    pst_bf = [psb[6].bitcast(BF16), psb[7].bitcast(BF16)]

    for i in range(8):
        S.add("dve", (lambda i: lambda e: e.memset(ub[i][:, 0:3], 0.0))(i), w=[("ub", i)])

    st_ctr = [0]
    ps_ctr = [0]
    sm_ctr = [0]

    def next_st():
        i = st_ctr[0] % 4
        st_ctr[0] += 1
        return i

    def next_ps():
        i = ps_ctr[0] % 6
        ps_ctr[0] += 1
        return i

    def build_hT(xsrc, ntb=16):
        for tb in range(ntb):
            b = tb % 2
            S.add("sp", (lambda b, tb: lambda e: e.dma_start(out=xt[b][:, :], in_=xsrc[tb * 128:(tb + 1) * 128, :]))(b, tb),
                  w=[("xt", b)], dkey=f"xt{b}")
            so = (sm_ctr[0] % 8) * 4
            sm_ctr[0] += 1
            ss = sm[:, so:so + 1]
            rs = sm[:, so + 1:so + 2]
            S.add("act", (lambda b, ss: lambda e: e.activation(out=junk[:, :], in_=xt[b][:, :], func=AF.Square, accum_out=ss))(b, ss),
                  r=[("xt", b)], w=["junk", ("sm", so)])
            S.add("act", (lambda ss, rs: lambda e: e.activation(out=rs, in_=ss, func=AF.Sqrt, scale=1.0 / D, bias=epsc))(ss, rs),
                  r=[("sm", so), "cst"], w=[("sm", so + 1)])
            S.add("dve", (lambda rs: lambda e: e.reciprocal(out=rs, in_=rs))(rs),
                  r=[("sm", so + 1)], w=[("sm", so + 1)])
            S.add("dve", (lambda b, rs: lambda e: e.scalar_tensor_tensor(out=hn[b][:, :], in0=xt[b][:, :], scalar=rs, in1=cvec[:, V_GN:V_GN + D], op0=ALU.mult, op1=ALU.mult))(b, rs),
                  r=[("xt", b), ("sm", so + 1), "cvec"], w=[("hn", b)])
            for half in range(2):
                pb = 6 + half
                for j in range(8):
                    kc = half * 8 + j
                    S.add("pe", (lambda b, kc, j, half: lambda e: e.transpose(out=pst_bf[half][:, j * 128:(j + 1) * 128], in_=hn[b][:, kc * 128:(kc + 1) * 128], identity=ident))(b, kc, j, half),
                          r=[("hn", b), "cmat"], x=[PK(pb)])
                dst = hT[:, half * 8:(half + 1) * 8, tb * 128:(tb + 1) * 128]
                src = pst_bf[half][:, :].rearrange("p (j t) -> p j t", j=8)
                if half == 0:
                    S.add("act", (lambda dst, src: lambda e: e.copy(out=dst, in_=src))(dst, src), x=[PK(pb)], w=[("hT", tb, half)])
                else:
                    S.add("dve", (lambda dst, src: lambda e: e.tensor_copy(out=dst, in_=src))(dst, src), x=[PK(pb)], w=[("hT", tb, half)])

    def hT_keys(tbs):
        return [("hT", tb, h) for tb in tbs for h in range(2)]

    def load_w(g, cols):
        b = g % 2
        o = 0
        for (c0, n) in cols:
            src = win_d[:, c0:c0 + n].rearrange("(kc p) n -> p kc n", p=128)
            S.add("pool", (lambda b, o, n, src: lambda e: e.dma_start(out=wb[b][:, :, o:o + n], in_=src))(b, o, n, src),
                  w=[("wb", b)], dkey=f"wb{b}")
            o += n
        return b

    def mm_tok(b, tb, ncols):
        pi = next_ps()
        for kc in range(KC):
            S.add("pe", (lambda pi, kc, tb, b, ncols: lambda e: e.matmul(psb[pi][:, 0:ncols], lhsT=hT[:, kc, tb * 128:(tb + 1) * 128], rhs=wb[b][:, kc, 0:ncols], start=(kc == 0), stop=(kc == KC - 1)))(pi, kc, tb, b, ncols),
                  r=hT_keys([tb]) + [("wb", b)], x=[PK(pi)])
        return pi

    def mm_feat(b, ct, sb):
        pi = next_ps()
        for kc in range(KC):
            S.add("pe", (lambda pi, kc, ct, sb, b: lambda e: e.matmul(psb[pi][:, :], lhsT=wb[b][:, kc, ct * 128:(ct + 1) * 128], rhs=hT[:, kc, sb * 512:(sb + 1) * 512], start=(kc == 0), stop=(kc == KC - 1)))(pi, kc, ct, sb, b),
                  r=hT_keys(range(sb * 4, sb * 4 + 4)) + [("wb", b)], x=[PK(pi)])
        return pi

    def evac_store(pi, dst_ap, func=None, eng="dve"):
        si = next_st()
        if func is None:
            if eng == "dve":
                S.add("dve", (lambda pi, si: lambda e: e.tensor_copy(out=st[si][:, :], in_=psb[pi][:, :]))(pi, si), x=[PK(pi)], w=[("st", si)])
            else:
                S.add("act", (lambda pi, si: lambda e: e.copy(out=st[si][:, :], in_=psb[pi][:, :]))(pi, si), x=[PK(pi)], w=[("st", si)])
        else:
            S.add("act", (lambda pi, si, func: lambda e: e.activation(out=st[si][:, :], in_=psb[pi][:, :], func=func))(pi, si, func), x=[PK(pi)], w=[("st", si)])
        S.add("sp", (lambda si, dst_ap: lambda e: e.dma_start(out=dst_ap, in_=st[si][:, :]))(si, dst_ap), r=[("st", si)], dkey=f"st{si}")

    def conv_evac(pi, ui, dst_ap, parity):
        u = ub[ui]
        ca = cacc[parity]
        cw = lambda j: cvec[:, V_CW + ui * 4 + j:V_CW + ui * 4 + j + 1]
        cb = cvec[:, V_CB + ui:V_CB + ui + 1]
        S.add("act", (lambda pi, u: lambda e: e.copy(out=u[:, 3:515], in_=psb[pi][:, :]))(pi, u), x=[PK(pi)], w=[("ubm", ui)])
        S.add("dve", (lambda u, ca: lambda e: e.tensor_scalar(out=ca[:, :], in0=u[:, 3:515], scalar1=cw(3), scalar2=cb, op0=ALU.mult, op1=ALU.add))(u, ca),
              r=[("ubm", ui), "cvec"], w=[("cacc", parity)])
        for j in (2, 1, 0):
            S.add("dve", (lambda u, ca, j: lambda e: e.scalar_tensor_tensor(out=ca[:, :], in0=u[:, j:j + 512], scalar=cw(j), in1=ca[:, :], op0=ALU.mult, op1=ALU.add))(u, ca, j),
                  r=[("ubm", ui), ("ub", ui), ("cacc", parity)], w=[("cacc", parity)])
        S.add("dve", (lambda u: lambda e: e.tensor_copy(out=u[:, 0:3], in_=u[:, 512:515]))(u), r=[("ubm", ui)], w=[("ub", ui)])
        si = next_st()
        S.add("act", (lambda ca, si: lambda e: e.activation(out=st[si][:, :], in_=ca[:, :], func=AF.Silu))(ca, si), r=[("cacc", parity)], w=[("st", si)])
        S.add("sp", (lambda si, dst_ap: lambda e: e.dma_start(out=dst_ap, in_=st[si][:, :]))(si, dst_ap), r=[("st", si)], dkey=f"st{si}")

    gctr = [0]

    def full_pass(ps_id):
        tok0 = ps_id * 2048
        build_hT(x_d[tok0:tok0 + 2048, :])
        b = load_w(gctr[0], [(C_FF, 8), (C_MI, 16)])
        gctr[0] += 1
        for tb in range(16):
            pi = mm_tok(b, tb, 24)
            gtb = ps_id * 16 + tb
            S.add("dve", (lambda pi, gtb: lambda e: e.tensor_copy(out=graw[:, gtb, :], in_=psb[pi][:, 0:24]))(pi, gtb), x=[PK(pi)], w=[("graw", gtb)])
        for gi in range(2):
            b = load_w(gctr[0], [(C_FK + gi * 512, 512)])
            gctr[0] += 1
            for ct in range(4):
                h = gi * 4 + ct
                for sb in range(4):
                    pi = mm_feat(b, ct, sb)
                    evac_store(pi, kT_d[h, :, tok0 + sb * 512:tok0 + (sb + 1) * 512], eng=("dve" if sb % 2 else "act"))
        for gi, (c0, dd) in enumerate(((C_MQ, qmT_d), (C_MK, kmT_d))):
            b = load_w(gctr[0], [(c0, 512)])
            gctr[0] += 1
            for ct in range(4):
                for sb in range(4):
                    pi = mm_feat(b, ct, sb)
                    conv_evac(pi, gi * 4 + ct, dd[ct, :, tok0 + sb * 512:tok0 + (sb + 1) * 512], sb % 2)
        for (c0, dd) in ((C_FV, vF_d), (C_MV, vM_d)):
            for gi in range(2):
                b = load_w(gctr[0], [(c0 + gi * 512, 512)])
                gctr[0] += 1
                for tb in range(16):
                    pi = mm_tok(b, tb, 512)
                    evac_store(pi, dd[tok0 + tb * 128:tok0 + (tb + 1) * 128, gi * 512:(gi + 1) * 512], eng=("dve" if tb % 2 else "act"))

    def own_pass():
        build_hT(xo_d[:, :])
        for gi in range(2):
            b = load_w(gctr[0], [(C_FQ + gi * 512, 512)])
            gctr[0] += 1
            for ct in range(4):
                h = gi * 4 + ct
                for sb in range(4):
                    pi = mm_feat(b, ct, sb)
                    evac_store(pi, qT_d[h, :, sb * 512:(sb + 1) * 512], eng="dve")
        for (c0, dd, fn) in ((C_FZ, gz_d, AF.Silu), (C_MO, go_d, AF.Sigmoid), (C_MZ, gmz_d, AF.Silu)):
            for gi in range(2):
                b = load_w(gctr[0], [(c0 + gi * 512, 512)])
                gctr[0] += 1
                for tb in range(16):
                    pi = mm_tok(b, tb, 512)
                    evac_store(pi, dd[tb * 128:(tb + 1) * 128, gi * 512:(gi + 1) * 512], func=fn)

    full_pass(0)
    full_pass(1)
    own_pass()
    S.flush(barrier=True)
    M.reset(p_mark)

    yT_off = (M.top - KC * TO * 2) // 64 * 64
    yT = nc.alloc_sbuf_tensor_at("yT", [128, KC, TO], BF16, offset=yT_off)
    M.top = yT_off
    mask = cmat[:, M_MASK:M_MASK + 4096].rearrange("p (r t) -> p r t", r=8)
    sel0 = cvec[:, V_SEL:V_SEL + 1]
    sel1 = cvec[:, V_SEL + 1:V_SEL + 2]
    onec = cst[:, 1:2]
    nln8 = cst[:, 2:3]
    pstb = psb[7].bitcast(BF16)

    zg = M.alloc("zg", [128, 32, 24], F32)
    el = M.alloc("el", [128, 32, 24], F32)
    cum = M.alloc("cum", [128, 32, 24], F32)
    wi = M.alloc("wi", [128, 32, 24], F32)
    G = M.alloc("G", [128, 33, 24], F32)
    biasF = M.alloc("biasF", [128, 4, 32, 8], F32)
    refF = M.alloc("refF", [128, 4, 8], F32)
    t8 = M.alloc("t8", [128, 4, 8], F32)
    uu = M.alloc("uu", [128, 32, 8], F32)
    EX = M.alloc("EX", [128, 1088], F32)
    reb = EX[:, 832:1088].rearrange("p (a b) -> p a b", a=32)
    rebo = M.alloc("rebo", [128, 4, 4, 8], F32)
    wv = EX[:, 0:256].rearrange("p (a b) -> p a b", a=32)
    wh = EX[:, 256:512].rearrange("p (a b) -> p a b", a=32)
    eb = EX[:, 512:768].rearrange("p (a b) -> p a b", a=32)
    dC = EX[:, 768:800].rearrange("p (a b) -> p a b", a=4)
    dH = EX[:, 800:832].rearrange("p (a b) -> p a b", a=4)
    ebo = M.alloc("ebo", [128, 4, 4, 8], F32)
    ebt = M.alloc("ebt", [128, 4, 4, 8], F32)
    g_mark = M.mark()

    flat = lambda t: t[:, :, :].rearrange("p a b -> p (a b)")
    S.add("dve", lambda e: e.tensor_tensor(out=flat(zg), in0=flat(graw), in1=cvec[:, V_GB:V_GB + 768], op=ALU.add),
          r=["cvec"] + [("graw", i) for i in range(32)], w=["zg"])
    S.add("act", lambda e: e.activation(out=flat(el), in_=flat(zg), func=AF.Exp, scale=-1.0), r=["zg"], w=["el"])
    S.add("act", lambda e: e.activation(out=flat(el), in_=flat(el), func=AF.Ln, bias=onec, scale=1.0), r=["el", "cst"], w=["el"])
    tri = cvec[:, V_TRI:V_TRI + 128]
    ones = cvec[:, V_ONE:V_ONE + 128]
    for half in range(2):
        S.add("pe", (lambda half: lambda e: e.matmul(psb[half][:, 0:384], lhsT=ones, rhs=flat(el)[:, half * 384:(half + 1) * 384], start=True, stop=True))(half),
              r=["el", "cvec"], x=[PK(half)])
        S.add("pe", (lambda half: lambda e: e.matmul(psb[2 + half][:, 0:384], lhsT=tri, rhs=flat(el)[:, half * 384:(half + 1) * 384], start=True, stop=True))(half),
              r=["el", "cvec"], x=[PK(2 + half)])
        S.add("dve", (lambda half: lambda e: e.tensor_copy(out=flat(wi)[:, half * 384:(half + 1) * 384], in_=psb[half][:, 0:384]))(half), x=[PK(half)], w=[("wi", half)])
        S.add("dve", (lambda half: lambda e: e.tensor_copy(out=flat(cum)[:, half * 384:(half + 1) * 384], in_=psb[2 + half][:, 0:384]))(half), x=[PK(2 + half)], w=[("cumh", half)])
    S.add("dve", lambda e: e.memset(G[:, 0, :], 0.0), w=[("G", 0)])
    for kb in range(1, 33):
        S.add("dve", (lambda kb: lambda e: e.tensor_tensor(out=G[:, kb, :], in0=G[:, kb - 1, :], in1=wi[:, kb - 1, :], op=ALU.add))(kb),
              r=[("G", kb - 1), ("wi", (kb - 1) // 16)], w=[("G", kb)])
    S.add("dve", lambda e: e.tensor_tensor(out=cum[:, :, :], in0=cum[:, :, :], in1=G[:, 0:32, :], op=ALU.add),
          r=[("cumh", 0), ("cumh", 1)] + [("G", k) for k in range(33)], w=["cum"])
    for m in range(4):
        S.add("dve", (lambda m: lambda e: e.tensor_scalar(out=t8[:, m, :], in0=G[:, 8 * m + 2, 0:8], scalar1=sel0, scalar2=None, op0=ALU.mult))(m),
              r=["cum", "cvec"], w=[("t8", m)])
        S.add("dve", (lambda m: lambda e: e.scalar_tensor_tensor(out=refF[:, m, :], in0=G[:, 8 * m + 6, 0:8], scalar=sel1, in1=t8[:, m, :], op0=ALU.mult, op1=ALU.add))(m),
              r=["cum", "cvec", ("t8", m)], w=[("refF", m)])
        for kb in range(8 * m + 8):
            if kb < 8 * m:
                S.add("dve", (lambda m, kb: lambda e: e.tensor_tensor(out=biasF[:, m, kb, :], in0=cum[:, kb, 0:8], in1=refF[:, m, :], op=ALU.subtract))(m, kb),
                      r=["cum", ("refF", m)], w=["biasF"])
            else:
                kl = cvec[:, V_KILL + kb - 8 * m:V_KILL + kb - 8 * m + 1]
                S.add("dve", (lambda m, kb, kl: lambda e: e.scalar_tensor_tensor(out=biasF[:, m, kb, :], in0=cum[:, kb, 0:8], scalar=kl, in1=refF[:, m, :], op0=ALU.add, op1=ALU.subtract))(m, kb, kl),
                      r=["cum", ("refF", m), "cvec"], w=["biasF"])
    S.add("dve", lambda e: e.tensor_tensor(out=uu[:, :, :], in0=zg[:, :, 8:16], in1=cum[:, :, 16:24], op=ALU.add), r=["zg", "cum"], w=["uu"])
    for m in range(4):
        for kb in range(8 * m, 8 * m + 8):
            S.add("dve", (lambda m, kb: lambda e: e.tensor_tensor(out=wv[:, kb, :], in0=uu[:, kb, :], in1=G[:, 8 * m + 4, 16:24], op=ALU.subtract))(m, kb),
                  r=["uu", "cum"], w=["EX"])
            S.add("dve", (lambda m, kb: lambda e: e.tensor_tensor(out=wh[:, kb, :], in0=uu[:, kb, :], in1=G[:, 8 * m + 8, 16:24], op=ALU.subtract))(m, kb),
                  r=["uu", "cum"], w=["EX"])
            S.add("dve", (lambda m, kb: lambda e: e.tensor_tensor(out=eb[:, kb, :], in0=G[:, 8 * m + 4, 16:24], in1=cum[:, kb, 16:24], op=ALU.subtract))(m, kb),
                  r=["cum"], w=["EX"])
            S.add("dve", (lambda m, kb: lambda e: e.tensor_tensor(out=reb[:, kb, :], in0=cum[:, kb, 16:24], in1=G[:, 8 * m + 4, 16:24], op=ALU.subtract))(m, kb),
                  r=["cum"], w=["EX"])
        S.add("dve", (lambda m: lambda e: e.tensor_tensor(out=dC[:, m, :], in0=G[:, 8 * m, 16:24], in1=G[:, 8 * m + 8, 16:24], op=ALU.subtract))(m), r=["cum"], w=["EX"])
        S.add("dve", (lambda m: lambda e: e.tensor_tensor(out=dH[:, m, :], in0=G[:, 8 * m, 16:24], in1=G[:, 8 * m + 4, 16:24], op=ALU.subtract))(m), r=["cum"], w=["EX"])
    S.add("act", lambda e: e.activation(out=EX[:, 0:512], in_=EX[:, 0:512], func=AF.Exp, bias=nln8, scale=1.0), r=["EX", "cst"], w=["EX"])
    S.add("act", lambda e: e.activation(out=EX[:, 512:1088], in_=EX[:, 512:1088], func=AF.Exp), r=["EX"], w=["EX"])
    for m in range(4):
        S.add("dve", (lambda m: lambda e: e.tensor_scalar(out=ebt[:, m, :, :], in0=eb[:, 8 * m:8 * m + 4, :], scalar1=sel0, scalar2=None, op0=ALU.mult))(m),
              r=["EX", "cvec"], w=[("ebt", m)])
        S.add("dve", (lambda m: lambda e: e.scalar_tensor_tensor(out=ebo[:, m, :, :], in0=eb[:, 8 * m + 4:8 * m + 8, :], scalar=sel1, in1=ebt[:, m, :, :], op0=ALU.mult, op1=ALU.add))(m),
              r=["EX", "cvec", ("ebt", m)], w=["ebo"])
    for m in range(4):
        S.add("dve", (lambda m: lambda e: e.tensor_scalar(out=ebt[:, m, :, :], in0=reb[:, 8 * m:8 * m + 4, :], scalar1=sel0, scalar2=None, op0=ALU.mult))(m),
              r=["EX", "cvec", "ebo"], w=[("ebt", m)])
        S.add("dve", (lambda m: lambda e: e.scalar_tensor_tensor(out=rebo[:, m, :, :], in0=reb[:, 8 * m + 4:8 * m + 8, :], scalar=sel1, in1=ebt[:, m, :, :], op0=ALU.mult, op1=ALU.add))(m),
              r=["EX", "cvec", ("ebt", m)], w=["rebo"])
    S.flush(barrier=True)

    if DEBUG:
        dl = [(0, graw), (1, zg), (2, el), (3, cum), (4, wi)]
        for i, t in dl:
            S.add("sp", (lambda i, t: lambda e: e.dma_start(out=dbg_g[:, i, :], in_=flat(t)))(i, t), dkey="dbg")
        S.add("sp", lambda e: e.dma_start(out=dbg_g[:, 5, :], in_=G[:, 0:32, :].rearrange("p a b -> p (a b)")), dkey="dbg")
        S.add("sp", lambda e: e.dma_start(out=dbg_g[:, 6, :], in_=EX[:, 0:768]), dkey="dbg")
        S.add("sp", lambda e: e.dma_start(out=dbg_g[:, 7, 0:64], in_=EX[:, 768:832]), dkey="dbg")
        S.add("sp", lambda e: e.dma_start(out=dbg_g[:, 7, 64:192], in_=ebo[:, :, :, :].rearrange("p a b c -> p (a b c)")), dkey="dbg")
        for m in range(4):
            S.add("sp", (lambda m: lambda e: e.dma_start(out=dbg_g[:, 8 + m, 0:256], in_=biasF[:, m, :, :].rearrange("p a b -> p (a b)")))(m), dkey="dbg")
        S.flush(barrier=True)
    if STOP_AFTER in ("A", "B"):
        S.finish()
        return nc, S

    def MM(out, lhsT, rhs, start, stop, r, x):
        S.add("pe", lambda e: e.matmul(out, lhsT=lhsT, rhs=rhs, start=start, stop=stop), r=r, x=x)

    def TR(out, in_, r, x):
        S.add("pe", lambda e: e.transpose(out=out, in_=in_, identity=ident), r=list(r) + ["cmat"], x=x)

    def ACT(out, in_, func, r=(), w=(), x=(), **kw):
        S.add("act", lambda e: e.activation(out=out, in_=in_, func=func, **kw), r=r, w=w, x=x)

    def TS(eng, out, in0, s1, s2, op0, op1, r=(), w=(), x=()):
        if op1 is None:
            S.add(eng, lambda e: e.tensor_scalar(out=out, in0=in0, scalar1=s1, scalar2=None, op0=op0), r=r, w=w, x=x)
        else:
            S.add(eng, lambda e: e.tensor_scalar(out=out, in0=in0, scalar1=s1, scalar2=s2, op0=op0, op1=op1), r=r, w=w, x=x)

    def STT(eng, out, in0, scalar, in1, op0, op1, r=(), w=(), x=()):
        S.add(eng, lambda e: e.scalar_tensor_tensor(out=out, in0=in0, scalar=scalar, in1=in1, op0=op0, op1=op1), r=r, w=w, x=x)

    def TT(eng, out, in0, in1, op, r=(), w=(), x=()):
        S.add(eng, lambda e: e.tensor_tensor(out=out, in0=in0, in1=in1, op=op), r=r, w=w, x=x)

    def CP(eng, out, in_, r=(), w=(), x=()):
        S.add(eng, lambda e: e.tensor_copy(out=out, in_=in_), r=r, w=w, x=x)

    def RCP(out, in_, r=(), w=(), x=()):
        S.add("dve", lambda e: e.reciprocal(out=out, in_=in_), r=r, w=w, x=x)

    def DMA(eng, out, in_, dkey, r=(), w=()):
        S.add(eng, lambda e: e.dma_start(out=out, in_=in_), r=r, w=w, dkey=dkey)

    def MS(eng, ap, val, w):
        S.add(eng, lambda e: e.memset(ap, val), w=w)

    SCALE = 128.0 ** -0.5
    kTh = [M.alloc(f"kTh{i}", [128, T], BF16) for i in range(2)]
    vh = [M.alloc(f"vh{i}", [128, 32, 132], BF16) for i in range(2)]
    qTh = [M.alloc(f"qTh{i}", [128, TO], BF16) for i in range(2)]
    gzh = [M.alloc(f"gzh{i}", [128, 16, 128], BF16) for i in range(2)]
    pT = [M.alloc(f"pT{i}", [128, 512], BF16) for i in range(4)]
    ctr = dict(pT=0, t1=0, yb=0, sm=0, hb=0, o4=0)

    def rot(name, n):
        i = ctr[name] % n
        ctr[name] += 1
        return i

    def alloc_small(tag):
        d = {}
        d["t1"] = [M.alloc(f"t1{tag}{i}", [128, 128], F32) for i in range(4)]
        d["hb"] = [M.alloc(f"hb{tag}{i}", [128, 128], F32) for i in range(4)]
        d["yb"] = [M.alloc(f"yb{tag}{i}", [128, 128], BF16) for i in range(8)]
        d["junk"] = M.alloc(f"junk2{tag}", [128, 128], F32)
        d["sm"] = M.alloc(f"sm2{tag}", [128, 64], F32)
        d["o4"] = [M.alloc(f"o4{tag}{i}", [128, 4, 132], F32) for i in range(2)]
        return d

    pend = []

    def advance_all():
        for g in list(pend):
            try:
                next(g)
            except StopIteration:
                pend.remove(g)

    def drain_all():
        while pend:
            advance_all()

    def epilogue4(sb, srcs, src_r, src_x, sc4, sc_keys, gates, gate_key, gvec_off, kc_out, m, yb_eng):
        sm2 = sb["sm"]
        so = rot("sm", 4) * 16
        slot = ("sm2", so)
        ssq4 = sm2[:, so + 4:so + 8]
        r4 = sm2[:, so + 8:so + 12]
        rr4 = sm2[:, so + 12:so + 16]
        for c in range(4):
            kw = dict(accum_out=ssq4[:, c:c + 1])
            if sc4 is not None:
                kw["scale"] = sc4[:, c:c + 1]
            ACT(sb["junk"][:, :], srcs[c], AF.Square, r=list(sc_keys) + list(src_r[c]), x=src_x[c], w=["junk2", (slot, "ssq")], **kw)
        ACT(r4, ssq4, AF.Sqrt, r=[(slot, "ssq"), "cst"], w=[(slot, "r")], scale=1.0 / 128, bias=epsc)
        yield
        RCP(r4, r4, r=[(slot, "r")], w=[(slot, "r")])
        if sc4 is not None:
            TT("dve", rr4, r4, sc4, ALU.mult, r=[(slot, "r")] + list(sc_keys), w=[(slot, "rr")])
            fin, fk = rr4, (slot, "rr")
        else:
            fin, fk = r4, (slot, "r")
        tis = []
        for c in range(4):
            ti = rot("t1", 4)
            tis.append(ti)
            STT("dve", sb["t1"][ti][:, :], srcs[c], fin[:, c:c + 1], cvec[:, gvec_off:gvec_off + 128], ALU.mult, ALU.mult,
                r=[fk, "cvec"] + list(src_r[c]), x=src_x[c], w=[("t1", ti)])
        yis = []
        for c in range(4):
            yi = rot("yb", 8)
            yis.append(yi)
            TT(yb_eng, sb["yb"][yi][:, :], sb["t1"][tis[c]][:, :], gates[c], ALU.mult, r=[("t1", tis[c]), gate_key], w=[("yb", yi)])
        yield
        for c in range(4):
            TR(pstb[:, c * 128:(c + 1) * 128], sb["yb"][yis[c]][:, :], r=[("yb", yis[c])], x=[PK(7)])
        CP("dve", yT[:, kc_out, m * 512:(m + 1) * 512], pstb[:, 0:512], x=[PK(7)], w=[("yT", kc_out, m)])

    sbD = alloc_small("d")
    for b in range(2):
        MS("pool", vh[b][:, :, 128:129], 1.0, w=[("vh1", b)])

    def fox_epi(b, h, m):
        oi = rot("o4", 2)
        O4 = sbD["o4"][oi]
        CP("dve", O4[:, :, 0:129], ps_all[:, 3:7, 0:129], x=[PK(3), PK(4), PK(5), PK(6)], w=[("o4", oi)])
        so = rot("sm", 4) * 16
        rl4 = sbD["sm"][:, so:so + 4]
        RCP(rl4, O4[:, :, 128], r=[("o4", oi)], w=[(("sm2", so), "rl")])
        yield
        yield from epilogue4(sbD, [O4[:, c, 0:128] for c in range(4)], [[("o4", oi)]] * 4, [[]] * 4,
                             rl4, [(("sm2", so), "rl")], [gzh[b][:, 4 * m + c, :] for c in range(4)], ("gzh", b), V_GFOX + h * 128, h, m, "dve")

    def fox_head(h):
        b = h % 2
        DMA("sp", kTh[b][:, :], kT_d[h, :, :], f"kTh{b}", w=[("kTh", b)])
        DMA("sp", qTh[b][:, :], qT_d[h, :, :], f"qTh{b}", w=[("qTh", b)])
        DMA("sp", vh[b][:, :, 0:128], vF_d[:, h * 128:(h + 1) * 128].rearrange("(kb p) d -> p kb d", p=128), f"vh{b}", w=[("vh", b)])
        DMA("sp", gzh[b][:, :, :], gz_d[:, h * 128:(h + 1) * 128].rearrange("(tb p) d -> p tb d", p=128), f"gzh{b}", w=[("gzh", b)])
        for m in range(4):
            nkb = 8 * m + 8

            def emit_S(kb):
                pi = kb % 3
                MM(psb[pi][:, :], kTh[b][:, kb * 128:(kb + 1) * 128], qTh[b][:, m * 512:(m + 1) * 512], True, True,
                   r=[("kTh", b), ("qTh", b)], x=[PK(pi)])

            emit_S(0)
            emit_S(1)
            for kb in range(nkb):
                if kb + 2 < nkb:
                    emit_S(kb + 2)
                pi = kb % 3
                pj = rot("pT", 4)
                ACT(pT[pj][:, :], psb[pi][:, :], AF.Exp, r=["biasF"], x=[PK(pi)], w=[("pT", pj)],
                    bias=biasF[:, m, kb, h:h + 1], scale=SCALE)
                if kb >= 8 * m:
                    TT("dve", pT[pj][:, :], pT[pj][:, :], mask[:, kb - 8 * m, :], ALU.mult, r=[("pT", pj), "cmat"], w=[("pT", pj)])
                for c in range(4):
                    MM(psb[3 + c][:, 0:129], pT[pj][:, c * 128:(c + 1) * 128], vh[b][:, kb, 0:129], kb == 0, kb == nkb - 1,
                       r=[("pT", pj), ("vh", b), ("vh1", b)], x=[PK(3 + c)])
                if kb in (1, 3, 5):
                    advance_all()
            drain_all()
            g = fox_epi(b, h, m)
            next(g)
            pend.append(g)

    for h in range(8):
        fox_head(h)
    drain_all()
    S.flush(barrier=True)
    M.reset(g_mark)
    if DEBUG and STOP_AFTER == "D":
        DMA("sp", dbg_y[:, :, :], yT[:, :, :], "dbg")
        S.flush(barrier=True)
    if STOP_AFTER == "D":
        S.finish()
        return nc, S

    kmT = M.alloc("kmT", [128, T], BF16)
    qmT = M.alloc("qmT", [128, T], BF16)
    qown = M.alloc("qown", [128, TO], BF16)
    qtmp = M.alloc("qtmp", [128, 512], BF16)
    Kh = M.alloc("Kh", [128, 32, 128], BF16)
    vmh = [M.alloc(f"vmh{i}", [128, 32, 132], BF16) for i in range(2)]
    goh = [M.alloc(f"goh{i}", [128, 16, 128], BF16) for i in range(2)]
    gmzh = [M.alloc(f"gmzh{i}", [128, 16, 128], BF16) for i in range(2)]
    aT = [M.alloc(f"aT{i}", [128, 512], BF16) for i in range(4)]
    Cst = M.alloc("Cst", [128, 132], F32)
    Ct = M.alloc("Ct", [128, 132], BF16)
    sbE = alloc_small("e")
    for b in range(2):
        MS("pool", vmh[b][:, :, 128:129], 1.0, w=[("vmh1", b)])

    def ml_epi(b, h, m):
        sm2 = sbE["sm"]
        so = rot("sm", 4) * 16
        a4 = sm2[:, so:so + 4]
        k_a = (("sm2", so), "rl")
        oi = rot("o4", 2)
        O4 = sbE["o4"][oi]
        CP("dve", O4[:, :, 0:129], ps_all[:, 3:7, 0:129], x=[PK(3), PK(4), PK(5), PK(6)], w=[("o4", oi)])
        yield
        STT("dve", a4, O4[:, :, 128], -1.0, O4[:, :, 128], ALU.mult, ALU.max, r=[("o4", oi)], w=[k_a])
        TT("dve", a4, a4, rebo[:, m, :, h], ALU.max, r=[k_a, "EXr"], w=[k_a])
        RCP(a4, a4, r=[k_a], w=[k_a])
        his = []
        for c in range(4):
            hi = rot("hb", 4)
            his.append(hi)
            STT("dve", sbE["hb"][hi][:, :], O4[:, c, 0:128], a4[:, c:c + 1], goh[b][:, 4 * m + c, :], ALU.mult, ALU.mult,
                r=[k_a, ("goh", b), ("o4", oi)], w=[("hb", hi)])
        yield from epilogue4(sbE, [sbE["hb"][hi][:, :] for hi in his], [[("hb", hi)] for hi in his], [[]] * 4,
                             None, [], [gmzh[b][:, 4 * m + c, :] for c in range(4)], ("gmzh", b), V_GML + h * 128, 8 + h, m, "pool")

    def mlstm_ct(ct):
        DMA("sp", kmT[:, :], kmT_d[ct, :, :], "kmT", w=["kmT"])
        DMA("sp", qmT[:, :], qmT_d[ct, :, :], "qmT", w=["qmT"])

        def select(m):
            TS("pool", qown[:, m * 512:(m + 1) * 512], qmT[:, 1024 * m:1024 * m + 512], sel0, None, ALU.mult, None,
               r=["qmT", "cvec"], w=[("qown", m)])
            TS("pool", qtmp[:, :], qmT[:, 1024 * m + 512:1024 * m + 1024], sel1, None, ALU.mult, None,
               r=["qmT", "cvec"], w=["qtmp"])
            TT("pool", qown[:, m * 512:(m + 1) * 512], qown[:, m * 512:(m + 1) * 512], qtmp[:, :], ALU.add,
               r=["qtmp", ("qown", m)], w=[("qown", m)])

        def khat(m):
            for j in range(8):
                tb = 8 * m + j
                TR(pstb[:, j * 128:(j + 1) * 128], kmT[:, tb * 128:(tb + 1) * 128], r=["kmT"], x=[PK(7)])
            for j in range(8):
                tb = 8 * m + j
                for hh in range(2):
                    h = 2 * ct + hh
                    ACT(Kh[:, tb, hh * 64:(hh + 1) * 64], pstb[:, j * 128 + hh * 64:j * 128 + (hh + 1) * 64], AF.Copy,
                        r=["EXr"], x=[PK(7)], w=[("Kh", tb)], scale=wh[:, tb, h:h + 1])

        select(0)
        for hh in range(2):
            h = 2 * ct + hh
            b = h % 2
            po = 64 * hh
            DMA("sp", vmh[b][:, :, 0:128], vM_d[:, h * 128:(h + 1) * 128].rearrange("(kb p) d -> p kb d", p=128), f"vmh{b}", w=[("vmh", b)])
            DMA("sp", goh[b][:, :, :], go_d[:, h * 128:(h + 1) * 128].rearrange("(tb p) d -> p tb d", p=128), f"goh{b}", w=[("goh", b)])
            DMA("sp", gmzh[b][:, :, :], gmz_d[:, h * 128:(h + 1) * 128].rearrange("(tb p) d -> p tb d", p=128), f"gmzh{b}", w=[("gmzh", b)])
            MS("dve", Cst[:, :], 0.0, w=["Cst"])
            for m in range(4):
                TS("dve", Ct[po:po + 64, 0:129], Cst[po:po + 64, 0:129], dH[po:po + 64, m, h:h + 1], None, ALU.mult, None,
                   r=["Cst", "EXr"], w=["Ct"])
                if hh == 0 and m < 3:
                    select(m + 1)
                for r_ in range(8):
                    kb = 8 * m + r_
                    pi = r_ % 2
                    MM(psb[pi][:, :], kmT[po:po + 64, kb * 128:(kb + 1) * 128], qown[po:po + 64, m * 512:(m + 1) * 512], True, True,
                       r=["kmT", ("qown", m)], x=[PK(pi)])
                    aj = rot("pT", 4)
                    STT("dve", aT[aj][:, :], psb[pi][:, :], wv[:, kb, h:h + 1], mask[:, r_, :], ALU.mult, ALU.mult,
                        r=["EXr", "cmat"], x=[PK(pi)], w=[("aT", aj)])
                    for c in range(4):
                        MM(psb[3 + c][:, 0:129], aT[aj][:, c * 128:(c + 1) * 128], vmh[b][:, kb, 0:129], r_ == 0, False,
                           r=[("aT", aj), ("vmh", b), ("vmh1", b)], x=[PK(3 + c)])
                    if r_ in (1, 3, 5, 6):
                        advance_all()
                for c in range(4):
                    MM(psb[3 + c][:, 0:129], qown[po:po + 64, m * 512 + c * 128:m * 512 + (c + 1) * 128], Ct[po:po + 64, 0:129], False, True,
                       r=[("qown", m), "Ct"], x=[PK(3 + c)])
                drain_all()
                g = ml_epi(b, h, m)
                next(g)
                pend.append(g)
                if m < 3:
                    if hh == 0:
                        khat(m)
                    for r_ in range(8):
                        kb = 8 * m + r_
                        MM(psb[2][:, 0:129], Kh[:, kb, :], vmh[b][:, kb, 0:129], r_ == 0, r_ == 7,
                           r=[("Kh", kb), ("vmh", b), ("vmh1", b)], x=[PK(2)])
                    STT("dve", Cst[po:po + 64, 0:129], Cst[po:po + 64, 0:129], dC[po:po + 64, m, h:h + 1], psb[2][po:po + 64, 0:129], ALU.mult, ALU.add,
                        r=["Cst", "EXr"], x=[PK(2)], w=["Cst"])

    for ct in range(4):
        mlstm_ct(ct)
    drain_all()
    S.flush(barrier=True)
    M.reset(p_mark)
    if DEBUG and STOP_AFTER == "E":
        DMA("sp", dbg_y[:, :, :], yT[:, :, :], "dbg")
        S.flush(barrier=True)
    if STOP_AFTER == "E":
        S.finish()
        return nc, S

    wo = M.alloc("wo", [128, KC, D], BF16)
    xr = [M.alloc(f"xr{i}", [128, D], F32) for i in range(2)]
    junk3 = M.alloc("junk3", [128, D], BF16)
    sm3 = M.alloc("sm3", [128, 16], F32)
    for n in range(4):
        DMA("pool", wo[:, :, n * 512:(n + 1) * 512], wout_d[:, n * 512:(n + 1) * 512].rearrange("(kc p) n -> p kc n", p=128), f"wo{n}", w=[("wo", n)])
    pctr = 0
    for tb in range(16):
        b = tb % 2
        DMA("sp", xr[b][:, :], xo_d[tb * 128:(tb + 1) * 128, :], f"xr{b}", w=[("xr", b)])
        for n in range(4):
            pi = pctr % 6
            pctr += 1
            for kc in range(KC):
                MM(psb[pi][:, :], yT[:, kc, tb * 128:(tb + 1) * 128], wo[:, kc, n * 512:(n + 1) * 512], kc == 0, kc == KC - 1,
                   r=[("wo", n), "yTall"], x=[PK(pi)])
            TT("dve", xr[b][:, n * 512:(n + 1) * 512], psb[pi][:, :], xr[b][:, n * 512:(n + 1) * 512], ALU.add,
               r=[("xr", b)], x=[PK(pi)], w=[("xr", b)])
        so = (tb % 4) * 4
        ss = sm3[:, so:so + 1]
        rs = sm3[:, so + 1:so + 2]
        ACT(junk3[:, :], xr[b][:, :], AF.Square, r=[("xr", b)], w=["junk3", ("sm3", so)], accum_out=ss)
        ACT(rs, ss, AF.Sqrt, r=[("sm3", so), "cst"], w=[("sm3", so + 1)], scale=1.0 / D, bias=epsc)
        RCP(rs, rs, r=[("sm3", so + 1)], w=[("sm3", so + 1)])
        STT("dve", xr[b][:, :], xr[b][:, :], rs, cvec[:, V_GF:V_GF + D], ALU.mult, ALU.mult,
            r=[("xr", b), ("sm3", so + 1), "cvec"], w=[("xr", b)])
        DMA("sp", out_d[tb * 128:(tb + 1) * 128, :], xr[b][:, :], f"xr{b}", r=[("xr", b)])
    if DEBUG:
        DMA("sp", dbg_y[:, :, :], yT[:, :, :], "dbg", r=["yTall"])

    S.finish()
    return nc, S


def _consts(p, norm_w, final_norm_w, fox_out_norm_w, mlstm_out_norm_w, fox_f_bias, conv_w, conv_b,
            mlstm_i_bias, mlstm_f_bias):
    cv = np.zeros((128, CV), np.float32)
    cv[:, V_GN:V_GN + 2048] = norm_w[None, :]
    cv[:, V_GF:V_GF + 2048] = final_norm_w[None, :]
    cv[:, V_GFOX:V_GFOX + 1024] = fox_out_norm_w[None, :]
    cv[:, V_GML:V_GML + 1024] = mlstm_out_norm_w[None, :]
    gb = np.concatenate([fox_f_bias, mlstm_i_bias, mlstm_f_bias]).astype(np.float32)
    cv[:, V_GB:V_GB + 768] = np.tile(gb, 32)[None, :]
    cw = conv_w.reshape(4, 8, 128)
    cv[:, V_CW:V_CW + 32] = cw.transpose(2, 1, 0).reshape(128, 32)
    cv[:, V_CB:V_CB + 8] = conv_b.reshape(8, 128).T
    cv[:, V_SEL] = 1.0 - p
    cv[:, V_SEL + 1] = float(p)
    if p == 0:
        cv[:, V_KILL + 4:V_KILL + 8] = -30000.0
    s = np.arange(128)
    cv[:, V_TRI:V_TRI + 128] = (s[:, None] <= s[None, :]).astype(np.float32)
    cv[:, V_ONE:V_ONE + 128] = 1.0
    cm = np.zeros((128, CM), np.float32)
    cm[:, M_ID:M_ID + 128] = np.eye(128, dtype=np.float32)
    mk = np.zeros((128, 8, 512), np.float32)
    t = np.arange(512)
    for r in range(8):
        kpos = r * 128 + s
        qpos = p * 512 + t
        mk[:, r, :] = (kpos[:, None] <= qpos[None, :]).astype(np.float32)
    cm[:, M_MASK:M_MASK + 4096] = mk.reshape(128, 4096)
    return cv, cm.astype(ml_dtypes.bfloat16)


_CACHE = {}


def kernel(x, norm_w, w_in, fox_f_bias, conv_w, conv_b, mlstm_i_bias, mlstm_f_bias,
           fox_out_norm_w, mlstm_out_norm_w, w_out, final_norm_w):
    x = np.asarray(x, np.float32)
    w_in0 = np.ascontiguousarray(np.asarray(w_in, np.float32)[0])
    w_out0 = np.ascontiguousarray(np.asarray(w_out, np.float32)[0])
    if "nc" not in _CACHE:
        _CACHE["nc"] = build_program()
    nc, S = _CACHE["nc"]
    in_maps = []
    for c in range(8):
        b, p = c // 2, c % 2
        cv, cm = _consts(p, np.asarray(norm_w, np.float32)[0], np.asarray(final_norm_w, np.float32),
                         np.asarray(fox_out_norm_w, np.float32)[0], np.asarray(mlstm_out_norm_w, np.float32)[0],
                         np.asarray(fox_f_bias, np.float32)[0], np.asarray(conv_w, np.float32)[0],
                         np.asarray(conv_b, np.float32)[0], np.asarray(mlstm_i_bias, np.float32)[0],
                         np.asarray(mlstm_f_bias, np.float32)[0])
        xo = np.concatenate([x[b, 512 * (2 * m + p):512 * (2 * m + p) + 512] for m in range(4)], axis=0)
        in_maps.append({"x": np.ascontiguousarray(x[b]), "xo": np.ascontiguousarray(xo), "w_in": w_in0,
                        "w_out": w_out0, "cvec": cv, "cmat": cm})
    res = run_bass_kernel_spmd(nc, in_maps, core_ids=list(range(8)))
    _CACHE["res"] = res
    out = np.zeros((4, T, D), np.float32)
    for c in range(8):
        b, p = c // 2, c % 2
        o = res.results[c]["out"]
        for m in range(4):
            out[b, 512 * (2 * m + p):512 * (2 * m + p) + 512] = o[m * 512:(m + 1) * 512]
    return out
```
